# Optimizing a Trainium2 kernel written in Bass

```python
import jax, jax.numpy as jnp
from jax import lax
import numpy as np

D_MODEL = 1024
BATCH = 8
SEQ = 4096
DEPTH = 1
DEC_BATCH = 16
DEC_SEQ = 16
PAST_LEN = 2048

CHUNK = 64
D_RNN = D_MODEL
N_LRU_BLOCKS = 16
LRU_BLOCK = D_RNN // N_LRU_BLOCKS
LRU_CONV = 4
LRU_C = 8.0
D_POOL = D_MODEL
POOL_WINDOWS = (2, 4, 8, 16)
N_POOL_GROUPS = len(POOL_WINDOWS)
POOL_GROUP = D_POOL // N_POOL_GROUPS
POOL_HIST = max(POOL_WINDOWS) - 1
N_BRANCH = 2
D_IN = D_RNN + D_POOL + N_BRANCH * D_MODEL
D_FF = 3 * D_MODEL
FFN_CONV = 3
EPS = 1e-6

kernel_name = 'griffin_pool_hybrid_stream_step'


def rmsnorm(x, g):
    xf = x.astype(jnp.float32)
    y = xf * lax.rsqrt(jnp.mean(xf * xf, axis=-1, keepdims=True) + EPS)
    return (y * g.astype(jnp.float32)).astype(x.dtype)


def causal_dwconv(x, prev, w, b):
    width = w.shape[0]
    t_len = x.shape[1]
    buf = jnp.concatenate([prev.astype(x.dtype), x], axis=1)
    y = buf[:, 0:t_len] * w[0]
    for k in range(1, width):
        y = y + buf[:, k:k + t_len] * w[k]
    return y + b, buf[:, -(width - 1):]


def rg_lru(u, h0, w_a, b_a, w_x, b_x, lam):
    bsz, t_len, _ = u.shape
    ub = u.reshape(bsz, t_len, N_LRU_BLOCKS, LRU_BLOCK)
    r = jax.nn.sigmoid(jnp.einsum('btnc,ncd->btnd', ub, w_a).reshape(bsz, t_len, D_RNN) + b_a)
    i = jax.nn.sigmoid(jnp.einsum('btnc,ncd->btnd', ub, w_x).reshape(bsz, t_len, D_RNN) + b_x)
    log_a = (-LRU_C * r.astype(jnp.float32)) * jax.nn.softplus(-lam.astype(jnp.float32))
    a = jnp.exp(log_a)
    mult = jnp.sqrt(-jnp.expm1(2.0 * log_a))
    b = mult * (i * u).astype(jnp.float32)
    b = b.at[:, 0].add(a[:, 0] * h0.astype(jnp.float32))

    def combine(left, right):
        a1, b1 = left
        a2, b2 = right
        return a1 * a2, a2 * b1 + b2

    _, h = lax.associative_scan(combine, (a, b), axis=1)
    return h.astype(u.dtype), h[:, -1]


def multiscale_pool(p, prev, offset, w_pool, scale):
    bsz, t_len, _ = p.shape
    buf = jnp.concatenate([prev.astype(p.dtype), p], axis=1)
    buf_f = buf.astype(jnp.float32)
    cs = jnp.concatenate([jnp.zeros((bsz, 1, D_POOL), jnp.float32), jnp.cumsum(buf_f, axis=1)], axis=1)
    end = cs[:, POOL_HIST + 1:]
    pos = offset + jnp.arange(t_len)
    outs = []
    for g, w in enumerate(POOL_WINDOWS):
        sl = slice(g * POOL_GROUP, (g + 1) * POOL_GROUP)
        start = POOL_HIST + 1 - w
        s = end[..., sl] - cs[:, start:start + t_len, sl]
        cnt = jnp.minimum(pos + 1, w).astype(jnp.float32)[None, :, None]
        outs.append(s / cnt)
    pooled = (jnp.concatenate(outs, axis=-1) - p.astype(jnp.float32)).astype(p.dtype)
    pooled = pooled.reshape(bsz, t_len, N_POOL_GROUPS, POOL_GROUP)
    y = jnp.einsum('btgc,gcd->btgd', pooled, w_pool).reshape(bsz, t_len, D_POOL) * scale
    return y, buf[:, -POOL_HIST:]


def conv_ffn(xn, prev, w_up, w_conv, b_conv, w_down):
    h = xn @ w_up
    h, new_buf = causal_dwconv(h, prev, w_conv, b_conv)
    g, v = h[..., :D_FF], h[..., D_FF:]
    return (jax.nn.gelu(g) * v) @ w_down, new_buf


def trunk_layer(x, h0, lru_buf, pool_buf, ffn_buf, offset,
                norm_mix, w_in, conv_lru_w, conv_lru_b, w_ra, b_ra, w_ix, b_ix, lru_lambda,
                w_pool, pool_scale, w_br_lru, w_br_pool, w_out,
                norm_ffn, w_up, conv_ffn_w, conv_ffn_b, w_down):
    bsz, t_len, _ = x.shape
    xn = rmsnorm(x, norm_mix)
    z = xn @ w_in
    x_rnn = z[..., :D_RNN]
    x_pool = z[..., D_RNN:D_RNN + D_POOL]
    gate_logits = z[..., D_RNN + D_POOL:].reshape(bsz, t_len, N_BRANCH, D_MODEL)
    u, new_lru_buf = causal_dwconv(x_rnn, lru_buf, conv_lru_w, conv_lru_b)
    h, h_last = rg_lru(u, h0, w_ra, b_ra, w_ix, b_ix, lru_lambda)
    pp, new_pool_buf = multiscale_pool(x_pool, pool_buf, offset, w_pool, pool_scale)
    gates = jax.nn.sigmoid(gate_logits)
    merged = gates[:, :, 0] * (h @ w_br_lru) + gates[:, :, 1] * (pp @ w_br_pool)
    x = x + merged @ w_out
    f, new_ffn_buf = conv_ffn(rmsnorm(x, norm_ffn), ffn_buf, w_up, conv_ffn_w, conv_ffn_b, w_down)
    x = x + f
    return x, h_last, new_lru_buf, new_pool_buf, new_ffn_buf


def run_stack(x, st_h, st_lru, st_pool, st_ffn, offset, layer_w, norm_final):
    hs, lrus, pools, ffns = [], [], [], []
    for l in range(DEPTH):
        x, h_last, lb, pb, fb = trunk_layer(x, st_h[l], st_lru[l], st_pool[l], st_ffn[l], offset,
                                           *[w[l] for w in layer_w])
        hs.append(h_last)
        lrus.append(lb)
        pools.append(pb)
        ffns.append(fb)
    return (rmsnorm(x, norm_final), jnp.stack(hs), jnp.stack(lrus), jnp.stack(pools), jnp.stack(ffns))


def setup_inputs(seed: int = 0) -> dict:
    key = jax.random.key(seed)
    ks = jax.random.split(key, 32)

    def nrm(k, shape, scale):
        return jax.random.normal(k, shape, jnp.float32) * scale

    u_lam = jax.random.uniform(ks[12], (DEPTH, D_RNN), jnp.float32, 0.9, 0.999)
    return {
        'x_prompt': nrm(ks[0], (BATCH, SEQ, D_MODEL), 1.0),
        'x_sample': nrm(ks[1], (DEC_BATCH, DEC_SEQ, D_MODEL), 1.0),
        'state_lru_h': nrm(ks[2], (DEPTH, DEC_BATCH, D_RNN), 0.5),
        'state_lru_conv': nrm(ks[3], (DEPTH, DEC_BATCH, LRU_CONV - 1, D_RNN), 1.0),
        'state_pool': nrm(ks[4], (DEPTH, DEC_BATCH, POOL_HIST, D_POOL), 1.0),
        'state_ffn_conv': nrm(ks[5], (DEPTH, DEC_BATCH, FFN_CONV - 1, 2 * D_FF), 1.0),
        'norm_mix': 1.0 + nrm(ks[6], (DEPTH, D_MODEL), 0.02),
        'w_in': nrm(ks[7], (DEPTH, D_MODEL, D_IN), D_MODEL ** -0.5),
        'conv_lru_w': nrm(ks[8], (DEPTH, LRU_CONV, D_RNN), 0.5),
        'conv_lru_b': nrm(ks[9], (DEPTH, D_RNN), 0.02),
        'w_ra': nrm(ks[10], (DEPTH, N_LRU_BLOCKS, LRU_BLOCK, LRU_BLOCK), LRU_BLOCK ** -0.5),
        'b_ra': nrm(ks[11], (DEPTH, D_RNN), 0.02),
        'w_ix': nrm(ks[13], (DEPTH, N_LRU_BLOCKS, LRU_BLOCK, LRU_BLOCK), LRU_BLOCK ** -0.5),
        'b_ix': nrm(ks[14], (DEPTH, D_RNN), 0.02),
        'lru_lambda': jnp.log(u_lam) - jnp.log1p(-u_lam),
        'w_pool': nrm(ks[15], (DEPTH, N_POOL_GROUPS, POOL_GROUP, POOL_GROUP), POOL_GROUP ** -0.5),
        'pool_scale': 1.0 + nrm(ks[16], (DEPTH, D_POOL), 0.02),
        'w_br_lru': nrm(ks[17], (DEPTH, D_RNN, D_MODEL), D_RNN ** -0.5),
        'w_br_pool': nrm(ks[18], (DEPTH, D_POOL, D_MODEL), D_POOL ** -0.5),
        'w_out': nrm(ks[19], (DEPTH, D_MODEL, D_MODEL), D_MODEL ** -0.5),
        'norm_ffn': 1.0 + nrm(ks[20], (DEPTH, D_MODEL), 0.02),
        'w_up': nrm(ks[21], (DEPTH, D_MODEL, 2 * D_FF), D_MODEL ** -0.5),
        'conv_ffn_w': nrm(ks[22], (DEPTH, FFN_CONV, 2 * D_FF), 0.5),
        'conv_ffn_b': nrm(ks[23], (DEPTH, 2 * D_FF), 0.02),
        'w_down': nrm(ks[24], (DEPTH, D_FF, D_MODEL), D_FF ** -0.5),
        'norm_final': 1.0 + nrm(ks[25], (D_MODEL,), 0.02),
    }


def reference(x_prompt, x_sample, state_lru_h, state_lru_conv, state_pool, state_ffn_conv,
              norm_mix, w_in, conv_lru_w, conv_lru_b, w_ra, b_ra, w_ix, b_ix, lru_lambda,
              w_pool, pool_scale, w_br_lru, w_br_pool, w_out,
              norm_ffn, w_up, conv_ffn_w, conv_ffn_b, w_down, norm_final):
    layer_w = (norm_mix, w_in, conv_lru_w, conv_lru_b, w_ra, b_ra, w_ix, b_ix, lru_lambda,
               w_pool, pool_scale, w_br_lru, w_br_pool, w_out,
               norm_ffn, w_up, conv_ffn_w, conv_ffn_b, w_down)
    dt = x_prompt.dtype
    p_h0 = jnp.zeros((DEPTH, BATCH, D_RNN), jnp.float32)
    p_lru0 = jnp.zeros((DEPTH, BATCH, LRU_CONV - 1, D_RNN), dt)
    p_pool0 = jnp.zeros((DEPTH, BATCH, POOL_HIST, D_POOL), dt)
    p_ffn0 = jnp.zeros((DEPTH, BATCH, FFN_CONV - 1, 2 * D_FF), dt)
    y_prompt, p_h, p_lru, p_pool, p_ffn = run_stack(x_prompt, p_h0, p_lru0, p_pool0, p_ffn0, 0,
                                                    layer_w, norm_final)
    y_sample, s_h, s_lru, s_pool, s_ffn = run_stack(x_sample, state_lru_h, state_lru_conv, state_pool,
                                                    state_ffn_conv, PAST_LEN, layer_w, norm_final)
    return (y_prompt, y_sample, p_h, p_lru, p_pool, p_ffn, s_h, s_lru, s_pool, s_ffn)
```

```python
import contextlib
import numpy as np
import concourse.bass as bass
import concourse.mybir as mybir
from concourse.bass_utils import run_bass_kernel_spmd

F32 = mybir.dt.float32
BF16 = mybir.dt.bfloat16
AF = mybir.ActivationFunctionType
ALU = mybir.AluOpType

ENGS = ("pe", "act", "dve", "pool", "sp")

D = 1024
KC = 8
DIN = 4096
DFF = 3072
NFFC = 48
SEQ = 4096
TT = 512
NPT = SEQ // TT
DEC = 16
EPS = 1e-6
NCORES = 8

V_G1, V_G2, V_CLW, V_CLB, V_BA, V_BX, V_LAM, V_PSC, V_CFW, V_CFB, NV = 0, 8, 16, 48, 56, 64, 72, 80, 88, 232, 280
DV_HBA, DV_HBX, DV_CH, DV_MH, DV_TMP, NDV = 0, 8, 16, 24, 25, 40

SAME_ENGINE_SYNC = True
HALO_IN_ENG = "act"
CAST_ENG = "dve"
PRIO = "prog"
TBL_AWARE = False
TAP_SRC = "sbuf"
DROP_S2 = False
FFN_TAP_V = "act"
FFN_TAP_G = "act"
COARSE_BUFS = ()
REORDER_OFF = ()
NS_E = 9
NS_U = 6
NS_L = 8
NSB = 4
NW = 7
NPS = 4


class Buf:
    __slots__ = ("name", "w", "r", "gen")

    def __init__(self, name):
        self.name = name
        self.w = None
        self.r = []
        self.gen = 0


class TB:
    __slots__ = ("t", "b", "gen")

    def __init__(self, t, b, gen=0):
        self.t = t
        self.b = b
        self.gen = gen


class CT:
    def __init__(self, t, name, n):
        self.t = t
        if name in COARSE_BUFS:
            b = Buf(name)
            self.c = [TB(t, b) for i in range(n)]
        else:
            self.c = [TB(t, Buf(f"{name}.{i}")) for i in range(n)]


class Op:
    __slots__ = ("idx", "eng", "fn", "preds", "dur", "dsem", "lat", "tbl")


DEF_DUR = {"pe": 1.8, "act": 0.65, "dve": 0.75, "pool": 0.45, "sp": 0.08}
SYNC_LAT = 0.25


class FW:
    def __init__(self, nc):
        self.nc = nc
        self.ops = []
        self.dsem_group = {}
        self.dsem_cnt = {}
        self.dord = {}
        self.sems = {}
        self.scale = 1.0

    def dma_sem(self, key, group=False):
        assert key not in self.dsem_group
        self.dsem_group[key] = group
        self.dsem_cnt[key] = 0
        return key

    def _add(self, e, fn, reads, writes, dur, dsem, lat):
        preds = set()
        for tb in reads:
            assert tb.gen == tb.b.gen, f"stale ring buffer {tb.b.name}"
            if tb.b.w is not None:
                preds.add(tb.b.w)
        for tb in writes:
            assert tb.gen == tb.b.gen, f"stale ring buffer {tb.b.name}"
            b = tb.b
            if b.w is not None:
                preds.add(b.w)
            preds.update(b.r)
        o = Op()
        o.idx = len(self.ops)
        o.eng = e
        o.fn = fn
        o.preds = preds
        o.dur = dur
        o.dsem = dsem
        o.lat = lat
        o.tbl = None
        self.ops.append(o)
        ws = set()
        for tb in writes:
            tb.b.w = o.idx
            tb.b.r = []
            ws.add(id(tb.b))
        for tb in reads:
            if id(tb.b) not in ws:
                tb.b.r.append(o.idx)
        return o

    def op(self, e, fn, reads=(), writes=(), d=None, tbl=None):
        if d is None:
            d = max(0.12, DEF_DUR[e] * self.scale)
        self._add(e, fn, reads, writes, d, None, 0.0).tbl = tbl

    def dma(self, q, fn, dsem, reads=(), writes=(), lat=2.0, nbytes=0):
        dur = max(DEF_DUR["sp"], nbytes / 360e3) if q == "sp" else 1.0
        o = self._add(q, fn, reads, writes, dur, dsem, lat)
        self.dsem_cnt[dsem] += 1
        self.dord[o.idx] = self.dsem_cnt[dsem]

    def final_wait(self, e, tbs):
        self._add(e, None, tbs, tbs, 0.01, None, 0.0)

    def schedule(self):
        import heapq
        ops = self.ops
        N = len(ops)
        members = {}
        for o in ops:
            if o.dsem is not None and self.dsem_group[o.dsem]:
                members.setdefault(o.dsem, []).append(o.idx)
        for o in ops:
            extra = set()
            for p in o.preds:
                ds = ops[p].dsem
                if ds is not None and self.dsem_group[ds] and o.dsem != ds:
                    extra.update(members[ds])
            o.preds |= extra
            if o.dsem is not None and self.dsem_group[o.dsem]:
                o.preds = set(p for p in o.preds if ops[p].dsem != o.dsem)
        succ = [[] for _ in range(N)]
        npred = [0] * N
        for o in ops:
            npred[o.idx] = len(o.preds)
            for p in o.preds:
                succ[p].append(o.idx)
        ready_t = [0.0] * N
        fin = [0.0] * N
        if PRIO == "blevel":
            bl = [0.0] * N
            for o in reversed(ops):
                m = 0.0
                for sidx in succ[o.idx]:
                    if bl[sidx] > m:
                        m = bl[sidx]
                bl[o.idx] = m + o.dur + o.lat
            key = [(-bl[i], i) for i in range(N)]
        else:
            key = [(i, i) for i in range(N)]
        free = {e: 0.0 for e in ENGS}
        avail = {e: [] for e in ENGS}
        now = {e: [] for e in ENGS}
        order = {e: [] for e in ENGS}
        cur_tbl = [None]
        for o in ops:
            if npred[o.idx] == 0:
                heapq.heappush(avail[o.eng], (0.0, key[o.idx], o.idx))
        done = 0

        def pick_now(e):
            nw = now[e]
            if e != "act" or not TBL_AWARE or len(nw) < 2:
                return nw[0]
            best = None
            for cand in heapq.nsmallest(5, nw):
                t = ops[cand[1]].tbl
                if t is None or t == cur_tbl[0]:
                    best = cand
                    break
            return best if best is not None else nw[0]

        while done < N:
            best = None
            for e in ENGS:
                av, nw, fr = avail[e], now[e], free[e]
                while av and av[0][0] <= fr:
                    it = heapq.heappop(av)
                    heapq.heappush(nw, (it[1], it[2]))
                if nw:
                    pk = pick_now(e)
                    cand = (fr, pk[0], e, 1, pk)
                elif av:
                    cand = (av[0][0], av[0][1], e, 0, av[0])
                else:
                    continue
                if best is None or (cand[0], cand[1]) < (best[0], best[1]):
                    best = cand
            assert best is not None, "scheduler stuck (cyclic deps?)"
            start, _, e, fromnow, item = best
            if fromnow:
                now[e].remove(item)
                heapq.heapify(now[e])
                i = item[1]
            else:
                heapq.heappop(avail[e])
                i = item[2]
            o = ops[i]
            if e == "act" and o.tbl is not None:
                if cur_tbl[0] is not None and cur_tbl[0] != o.tbl:
                    start += 1.28
                cur_tbl[0] = o.tbl
            free[e] = start + o.dur
            fin[i] = start + o.dur + o.lat
            order[e].append(i)
            done += 1
            for sidx in succ[i]:
                t = fin[i] + (SYNC_LAT if ops[sidx].eng != e or o.dsem is not None else 0.05)
                if t > ready_t[sidx]:
                    ready_t[sidx] = t
                npred[sidx] -= 1
                if npred[sidx] == 0:
                    heapq.heappush(avail[ops[sidx].eng], (ready_t[sidx], key[sidx], sidx))
        self.makespan = max(fin)
        for e in REORDER_OFF:
            order[e] = sorted(order[e])
        self.order = order
        return order

    def build(self, st):
        nc = self.nc
        ops = self.ops
        order = self.schedule()
        ev = {}
        for e in ENGS:
            n = 0
            for i in order[e]:
                o = ops[i]
                if o.dsem is None:
                    if o.fn is not None:
                        n += 1
                        ev[i] = (e, n)
                    else:
                        ev[i] = None
                else:
                    k = self.dsem_cnt[o.dsem] if self.dsem_group[o.dsem] else self.dord[i]
                    ev[i] = (o.dsem, 16 * k)
        streams = {e: [] for e in ENGS}
        for e in ENGS:
            waited = {}
            for i in order[e]:
                o = ops[i]
                need = {}
                for p in o.preds:
                    pe = ev[p]
                    if pe is None:
                        continue
                    k, v = pe
                    if k == e and not SAME_ENGINE_SYNC:
                        continue
                    if o.dsem is not None and k == o.dsem and self.dsem_group[k]:
                        continue
                    if need.get(k, 0) < v:
                        need[k] = v
                wl = []
                for k, v in need.items():
                    if waited.get(k, 0) < v:
                        waited[k] = v
                        wl.append((k, v))
                streams[e].append((i, wl))
        val = {}
        ptr = {e: 0 for e in ENGS}
        progress = True
        while progress:
            progress = False
            for e in ENGS:
                while ptr[e] < len(streams[e]):
                    i, wl = streams[e][ptr[e]]
                    if all(val.get(k, 0) >= v for k, v in wl):
                        o = ops[i]
                        if o.dsem is not None:
                            val[o.dsem] = val.get(o.dsem, 0) + 16
                        elif o.fn is not None:
                            val[e] = val.get(e, 0) + 1
                        ptr[e] += 1
                        progress = True
                    else:
                        break
        if not all(ptr[e] == len(streams[e]) for e in ENGS):
            msg = []
            for e in ENGS:
                if ptr[e] < len(streams[e]):
                    i, wl = streams[e][ptr[e]]
                    msg.append(f"{e}: op#{i} pos {ptr[e]}/{len(streams[e])} waits {[(k, v, val.get(k, 0)) for k, v in wl if val.get(k, 0) < v]} preds {sorted(ops[i].preds)[-6:]}")
            raise AssertionError("semaphore deadlock in generated program: " + " | ".join(msg))
        for k in list(ENGS) + list(self.dsem_group.keys()):
            self.sems[k] = st.enter_context(nc.semaphore("s_" + str(k)))
        block = st.enter_context(nc.Block())
        sems = self.sems

        def run(eng, e):
            for i, wl in streams[e]:
                for k, v in wl:
                    eng.wait_ge(sems[k], v)
                o = ops[i]
                if o.fn is None:
                    continue
                if o.dsem is not None:
                    o.fn(eng).then_inc(sems[o.dsem], 16)
                else:
                    o.fn(eng).then_inc(sems[e], 1)

        @block.sync
        def _(eng):
            run(eng, "sp")

        @block.tensor
        def _(eng):
            run(eng, "pe")

        @block.scalar
        def _(eng):
            run(eng, "act")

        @block.vector
        def _(eng):
            run(eng, "dve")

        @block.gpsimd
        def _(eng):
            run(eng, "pool")


class Ring:
    def __init__(self, tbs):
        self.tbs = tbs
        self.i = 0

    def grab(self):
        tb = self.tbs[self.i % len(self.tbs)]
        self.i += 1
        tb.b.gen += 1
        return TB(tb.t, tb.b, tb.b.gen)


def build_program(worder=None):
    nc = bass.Bass("TRN2", target_bir_lowering=False)
    fw = FW(nc)
    st = contextlib.ExitStack()

    def din(name, shape, dt=F32):
        return nc.dram_tensor(name, list(shape), dt, kind="ExternalInput").ap()

    def dout(name, shape, dt=F32):
        return nc.dram_tensor(name, list(shape), dt, kind="ExternalOutput").ap()

    def dint(name, shape, dt):
        return nc.dram_tensor(name, list(shape), dt).ap()

    xp_d = din("xp", [SEQ, D])
    xs_d = din("xs", [2 * DEC, D])
    sth_d = din("st_h", [128, 2, 8])
    str_d = din("st_r", [128, 2, 24])
    stp_d = din("st_p", [128, 2, 120])
    stf_d = din("st_f", [128, 2, 96])
    vecs_d = din("vecs", [128, NV])
    gf_d = din("gf_bc", [128, D])
    id_d = din("ident", [128, 128])
    corr_d = din("corr", [128, 64])
    w_in_d = din("w_in", [D, DIN])
    w_ra_d = din("w_ra", [16, 64, 64])
    w_ix_d = din("w_ix", [16, 64, 64])
    w_pool_d = din("w_pool", [4, 256, 256])
    w_brl_d = din("w_br_lru", [D, D])
    w_brp_d = din("w_br_pool", [D, D])
    w_out_d = din("w_out", [D, D])
    w_up_d = din("w_up", [D, 2 * DFF])
    w_down_d = din("w_down", [DFF, D])

    wb_in = dint("wb_in", [D, DIN], BF16)
    wb_brl = dint("wb_brl", [D, D], BF16)
    wb_brp = dint("wb_brp", [D, D], BF16)
    wb_out = dint("wb_out", [D, D], BF16)
    wb_up = dint("wb_up", [D, 2 * DFF], BF16)
    wb_down = dint("wb_down", [DFF, D], BF16)

    yp_d = dout("yp", [SEQ, D])
    ys_d = dout("ys", [2 * DEC, D])
    oh_d = dout("o_h", [128, 3, 8])
    or_d = dout("o_r", [128, 3, 24])
    op_d = dout("o_p", [128, 3, 120])
    of_d = dout("o_f", [128, 3, 96])

    def sb(name, shape, dt=F32):
        t = st.enter_context(nc.sbuf_tensor("sb_" + name, list(shape), dt))
        return TB(t, Buf(name))

    def ps(name, shape):
        t = st.enter_context(nc.psum_tensor("pp_" + name, list(shape), F32))
        return TB(t, Buf(name))

    vecs = sb("vecs", [128, NV])
    dv = sb("dv", [128, NDV])
    gf = sb("gf", [128, D])
    ident = sb("ident", [128, 128])
    corr = sb("corr", [128, 64])
    wra = sb("wra", [128, 8, 128], BF16)
    wix = sb("wix", [128, 8, 128], BF16)
    wpl = sb("wpl", [128, 4, 2, 256], BF16)
    states = []
    for q in range(2):
        ns_ = 1 + q
        a0 = sb(f"hst{q}", [128, ns_, 8])
        a1 = sb(f"hal_r{q}", [128, ns_, 8, 3])
        a2 = sb(f"hal_p{q}", [128, ns_, 8, 15])
        a3 = sb(f"hal_f{q}", [128, ns_, NFFC, 2])
        states.append((CT(a0.t, f"hst{q}", 8), CT(a1.t, f"hal_r{q}", 8), CT(a2.t, f"hal_p{q}", 8), CT(a3.t, f"hal_f{q}", NFFC)))
    xres = [[sb(f"xres{p}_{b}", [128, D]) for b in range(4)] for p in range(2)]
    dgr = Ring([sb(f"dg{i}", [128, 128]) for i in range(3)])
    NXT = 2
    xtmp = [sb(f"xtmp{i}", [128, D]) for i in range(NXT)]
    junk = sb("junk", [128, D], BF16)
    stat = Ring([sb(f"stat{i}", [128, 2]) for i in range(10)])
    statL = Ring([sb(f"statL{i}", [128, 2]) for i in range(6)])
    xnT = sb("xnT", [128, KC, TT], BF16); xnT = CT(xnT.t, "xnT", KC)
    x1nT = sb("x1nT", [128, KC, TT], BF16); x1nT = CT(x1nT.t, "x1nT", KC)
    hT = sb("hT", [128, KC, TT], BF16); hT = CT(hT.t, "hT", KC)
    ppT = sb("ppT", [128, KC, TT], BF16); ppT = CT(ppT.t, "ppT", KC)
    mgT = sb("mgT", [128, KC, TT], BF16); mgT = CT(mgT.t, "mgT", KC)
    actT = sb("actT", [128, DFF // 128, TT], BF16); actT = CT(actT.t, "actT", DFF // 128)
    S = Ring([sb(f"S{i}", [128, 528]) for i in range(NS_E)])
    SU = Ring([sb(f"SU{i}", [128, TT]) for i in range(NS_U)])
    SL = Ring([sb(f"SL{i}", [128, 516]) for i in range(NS_L)])
    SB_ = Ring([sb(f"SB{i}", [128, TT], BF16) for i in range(NSB)])
    plb = Ring([sb(f"plb{i}", [128, 2, TT], BF16) for i in range(2)])
    wslots = [sb(f"wsl{i}", [128, 2048], BF16) for i in range(NW)]
    wsem = [fw.dma_sem(f"w{i}") for i in range(NW)]
    PS = Ring([ps(f"ps{i}", [128, 512]) for i in range(NPS)])
    PSL = Ring([ps(f"psl{i}", [128, 512]) for i in range(NPS)])

    s_const = fw.dma_sem("const", group=True)
    s_cast = {k: fw.dma_sem("cast_" + k, group=(k == "small")) for k in ("small",)}
    s_x = [[fw.dma_sem(f"x{p}_{b}") for b in range(4)] for p in range(2)]
    s_xt = [fw.dma_sem(f"xt{i}") for i in range(NXT)]
    s_sm = [fw.dma_sem(f"small{i}") for i in range(3)]
    s_st = [fw.dma_sem(f"stin{q}", group=True) for q in range(3)]
    s_so = [fw.dma_sem(f"stout{q}", group=True) for q in range(3)]
    s_stf = [fw.dma_sem(f"stinf{q}") for q in range(3)]
    s_sof = [fw.dma_sem(f"stoutf{q}") for q in range(3)]

    scr = {}

    for tb, d in ((vecs, vecs_d), (gf, gf_d), (ident, id_d), (corr, corr_d)):
        fw.dma("sp", lambda e, tb=tb, d=d: e.dma_start(out=tb.t[:], in_=d), s_const, writes=[tb])

    cast_t = [0.0]

    def cast_piece(key, src, dst, c0, ncol):
        rows = src.shape[0]
        s2 = src[:, c0:c0 + ncol]
        d2 = dst[:, c0:c0 + ncol]
        if ncol > 1024:
            a = ncol // 1024
            s2 = s2.rearrange("k (a n) -> k a n", a=a)
            d2 = d2.rearrange("k (a n) -> k a n", a=a)
        cast_t[0] += rows * ncol * 6 / 330e3
        fw.dma("pool", lambda e: e.dma_start(out=d2, in_=s2), s_cast[key], writes=[scr[key]], lat=4.0 + cast_t[0])

    fw.op("dve", lambda e: e.memset(wra.t[:], 0.0), writes=[wra])
    fw.op("dve", lambda e: e.memset(wix.t[:], 0.0), writes=[wix])
    for (wt, wd, sk) in ((wra, w_ra_d, s_sm[0]), (wix, w_ix_d, s_sm[1])):
        for j in range(2):
            src = wd.rearrange("(c j) k d -> j k c d", j=2)[j]
            fw.dma("pool", lambda e, wt=wt, src=src, j=j: e.dma_start(out=wt.t[64 * j:64 * j + 64, :, 64 * j:64 * j + 64], in_=src),
                   sk, writes=[wt])
    fw.dma("pool", lambda e: e.dma_start(out=wpl.t[:], in_=w_pool_d.rearrange("g (kc p) n -> p g kc n", p=128)),
           s_sm[2], writes=[wpl])

    V = vecs.t
    DVt = dv.t
    fw.op("dve", lambda e: e.tensor_scalar(out=DVt[:, DV_HBA:DV_HBA + 8], in0=V[:, V_BA:V_BA + 8], scalar1=0.5, scalar2=None, op0=ALU.mult), reads=[vecs], writes=[dv])
    fw.op("dve", lambda e: e.tensor_scalar(out=DVt[:, DV_HBX:DV_HBX + 8], in0=V[:, V_BX:V_BX + 8], scalar1=0.5, scalar2=None, op0=ALU.mult), reads=[vecs], writes=[dv])
    fw.op("dve", lambda e: e.memset(DVt[:, DV_MH:DV_MH + 1], -0.5), writes=[dv])
    fw.op("act", lambda e: e.activation(out=DVt[:, DV_TMP:DV_TMP + 8], in_=V[:, V_LAM:V_LAM + 8], func=AF.Exp, scale=-1.0), reads=[vecs], writes=[dv])
    fw.op("act", lambda e: e.activation(out=DVt[:, DV_TMP:DV_TMP + 8], in_=DVt[:, DV_TMP:DV_TMP + 8], func=AF.Ln, bias=1.0, scale=1.0), reads=[dv], writes=[dv])
    fw.op("dve", lambda e: e.tensor_scalar(out=DVt[:, DV_CH:DV_CH + 8], in0=DVt[:, DV_TMP:DV_TMP + 8], scalar1=-4.0, scalar2=None, op0=ALU.mult), reads=[dv], writes=[dv])

    def vcol(c):
        return V[:, c:c + 1]

    def dcol(c):
        return DVt[:, c:c + 1]

    def kblock(mat, c0, nb):
        return mat.rearrange("(kc p) n -> p kc n", p=128)[:, :, c0:c0 + nb]

    def wdesc(key):
        kind = key[0]
        if kind in ("in", "brl", "brp", "up"):
            mb, mf = {"in": (wb_in, w_in_d), "brl": (wb_brl, w_brl_d), "brp": (wb_brp, w_brp_d), "up": (wb_up, w_up_d)}[kind]
            return (kblock(mb, 256 * key[1], 256), kblock(mf, 256 * key[1], 256), 8, 256)
        mb, mf = (wb_out, w_out_d) if kind == "out" else (wb_down, w_down_d)
        h, kg = key[1], key[2]
        sl = (slice(None), slice(4 * kg, 4 * kg + 4), slice(512 * h, 512 * h + 512))
        return (mb.rearrange("(kc p) n -> p kc n", p=128)[sl], mf.rearrange("(kc p) n -> p kc n", p=128)[sl], 4, 512)

    class WStream:
        def __init__(self, order):
            self.record = order is None
            self.order = [] if order is None else order
            self.issued = 0
            self.consumed = 0
            self.free = list(range(NW))
            self.loaded = {}

        def pump(self):
            if self.record:
                return
            while self.free and self.issued < len(self.order):
                key = self.order[self.issued]
                ap_b, ap_f, nk, nb = wdesc(key)
                slot = self.free.pop(0)
                tbs = wslots[slot]
                tbs.b.gen += 1
                tb = TB(tbs.t, tbs.b, tbs.b.gen)
                dst = tb.t[:, 0:nk * nb].rearrange("p (k n) -> p k n", n=nb)
                nby = 128 * nk * nb * 2
                if key not in scrb:
                    scrb[key] = TB(None, Buf("scr_" + str(key)))
                    fw.dma("pool", lambda e, dst=dst, ap_f=ap_f: e.dma_start(out=dst, in_=ap_f), s_wsw[slot], writes=[tb], lat=2.0 + 3 * nby / 360e3)
                    fw.dma("sp", lambda e, dst=dst, ap_b=ap_b: e.dma_start(out=ap_b, in_=dst), s_wb[slot], reads=[tb], writes=[scrb[key]], nbytes=nby)
                else:
                    fw.dma("sp", lambda e, dst=dst, ap_b=ap_b: e.dma_start(out=dst, in_=ap_b), wsem[slot], reads=[scrb[key]], writes=[tb], nbytes=nby)
                self.loaded[self.issued] = (slot, tb)
                self.issued += 1

        def next(self, key):
            if self.record:
                self.order.append(key)
                tbs = wslots[0]
                return (0, TB(tbs.t, tbs.b, tbs.b.gen))
            self.pump()
            assert self.order[self.consumed] == key, (self.order[self.consumed], key)
            slot, tb = self.loaded.pop(self.consumed)
            self.consumed += 1
            return (slot, tb)

        def done(self, h):
            if self.record:
                return
            self.free.append(h[0])
            self.pump()

    scrb = {}
    s_wb = [fw.dma_sem(f"wb{i}") for i in range(NW)]
    s_wsw = [fw.dma_sem(f"wsw{i}") for i in range(NW)]
    ws = WStream(worder)

    tsz = [TT] * (NPT - 1) + [TT // 2, TT // 2]
    ptiles = []
    r_ = 0
    for k, tz in enumerate(tsz):
        ptiles.append(dict(seq=0, T=tz, first=(k == 0), last=(k == len(tsz) - 1), xd=xp_d, yd=yp_d, r0=r_, prompt=True))
        r_ += tz
    for t_ in ptiles:
        t_["nseg"], t_["L"] = 1, t_["T"]
    stiles = [dict(seq=1, T=2 * DEC, first=True, last=True, xd=xs_d, yd=ys_d, r0=0, prompt=False, nseg=2, L=DEC)]
    tiles = ptiles[0:6] + [stiles[0]] + ptiles[6:]

    def seg3(ap2d, width, nseg):
        return ap2d.rearrange("p (s w) -> p s w", w=width)

    def blocks_of(T):
        return [(b * 128, min(128, T - b * 128)) for b in range((T + 127) // 128)]

    def norm_p1(xb, bs, ring):
        s = ring.grab()
        fw.op("act", lambda e: e.activation(out=junk.t[0:bs, :], in_=xb.t[0:bs, :], func=AF.Square, accum_out=s.t[0:bs, 0:1]),
              reads=[xb], writes=[junk, s], d=1.1)
        fw.op("dve", lambda e: e.tensor_scalar(out=s.t[0:bs, 0:1], in0=s.t[0:bs, 0:1], scalar1=1.0 / D, scalar2=EPS, op0=ALU.mult, op1=ALU.add),
              reads=[s], writes=[s], d=0.15)
        fw.op("pool", lambda e: e.tensor_tensor(out=s.t[0:bs, 1:2], in0=s.t[0:bs, 0:1], in1=DVt[0:bs, DV_MH:DV_MH + 1], op=ALU.pow),
              reads=[s, dv], writes=[s], d=0.5)
        return s

    def norm_p2(xb, bs, s, gcol, dstT, col0):
        dg = dgr.grab()
        fw.op("dve", lambda e: e.tensor_scalar(out=dg.t[0:bs, 0:bs], in0=ident.t[0:bs, 0:bs], scalar1=s.t[0:bs, 1:2], scalar2=None, op0=ALU.mult),
              reads=[s, ident], writes=[dg], d=0.15)
        pa = PS.grab()
        pb = PS.grab()

        def tr(e):
            for c in range(KC):
                pt = pa if c < 4 else pb
                i = e.matmul(pt.t[:, (c % 4) * 128:(c % 4) * 128 + bs], lhsT=xb.t[0:bs, c * 128:(c + 1) * 128], rhs=dg.t[0:bs, 0:bs], start=True, stop=True)
            return i
        fw.op("pe", tr, reads=[xb, dg], writes=[pa, pb], d=1.9 * (bs / 128.0) + 0.2)

        def ev_d(e):
            for c in range(0, 4):
                i = e.tensor_scalar(out=dstT.t[:, c, col0:col0 + bs], in0=pa.t[:, c * 128:c * 128 + bs], scalar1=vcol(gcol + c), scalar2=None, op0=ALU.mult)
            return i

        def ev_a(e):
            for c in range(4, KC):
                i = e.activation(out=dstT.t[:, c, col0:col0 + bs], in_=pb.t[:, (c - 4) * 128:(c - 4) * 128 + bs], func=AF.Identity, scale=vcol(gcol + c))
            return i
        fw.op("dve", ev_d, reads=[pa, vecs], writes=[dstT.c[c] for c in range(0, 4)], d=1.1 * fw.scale + 0.1)
        fw.op("act", ev_a, reads=[pb, vecs], writes=[dstT.c[c] for c in range(4, KC)], d=1.1 * fw.scale + 0.1)

    def load_tile(ti, par):
        T = ti["T"]
        for b, (c0, bs) in enumerate(blocks_of(T)):
            xb = xres[par][b]
            src = ti["xd"][ti["r0"] + c0: ti["r0"] + c0 + bs, :]
            fw.dma("sp", lambda e, xb=xb, src=src, bs=bs: e.dma_start(out=xb.t[0:bs, :], in_=src), s_x[par][b], writes=[xb], nbytes=bs * D * 4)

    def proj_fm(wtb, cs, srcT, T, ring):
        p = ring.grab()
        wv = wtb.t[:, 0:2048].rearrange("p (k n) -> p k n", n=256)

        def mm(e):
            for k in range(KC):
                i = e.matmul(p.t[:, 0:T], lhsT=wv[:, k, cs:cs + 128], rhs=srcT.t[:, k, 0:T], start=(k == 0), stop=(k == KC - 1))
            return i
        fw.op("pe", mm, reads=[wtb] + srcT.c, writes=[p], d=0.22 * 8 * fw.scale + 0.1)
        return p

    def tokproj(ti, par, srcT, nk, wkey, epi, ring):
        T = ti["T"]
        blks = blocks_of(T)
        cost = 0.22 * 4 * len(blks) if T > 32 else 1.5
        for h in range(2):
            pss = [ring.grab() for _ in blks]
            for kg in range(nk // 4):
                wh = ws.next((wkey, h, kg))
                wv = wh[1].t[:, 0:2048].rearrange("p (k n) -> p k n", n=512)

                def mm(e, kg=kg, wv=wv, pss=pss):
                    for kk in range(4):
                        kc = kg * 4 + kk
                        for b, (c0, bs) in enumerate(blks):
                            i = e.matmul(pss[b].t[0:bs, 0:512], lhsT=srcT.t[:, kc, c0:c0 + bs], rhs=wv[:, kk, :], start=(kc == 0), stop=(kc == nk - 1))
                    return i
                fw.op("pe", mm, reads=[wh[1]] + srcT.c[4 * kg:4 * kg + 4], writes=pss, d=0.22 * 4 * len(blks) * fw.scale + 0.1)
                ws.done(wh)
                yield cost
            for b, (c0, bs) in enumerate(blks):
                epi(b, bs, h, pss[b])
            yield 0.1

    def init_state_early(ti):
        hst, hal_r, hal_p, hal_f = states[0 if ti["prompt"] else 1]
        if ti["prompt"]:
            fw.op("pool", lambda e: e.memset(hst.t[:], 0.0), writes=hst.c, d=0.1)
            fw.op("pool", lambda e: e.memset(hal_r.t[:], 0.0), writes=hal_r.c, d=0.1)
            fw.op("pool", lambda e: e.memset(hal_p.t[:], 0.0), writes=hal_p.c, d=0.15)
        else:
            fw.dma("sp", lambda e: e.dma_start(out=hst.t[:], in_=sth_d[:, :, :]), s_st[ti["seq"]], writes=hst.c)
            fw.dma("sp", lambda e: e.dma_start(out=hal_r.t[:].rearrange("p s c k -> p s (c k)"), in_=str_d[:, :, :]), s_st[ti["seq"]], writes=hal_r.c)
            fw.dma("sp", lambda e: e.dma_start(out=hal_p.t[:].rearrange("p s c k -> p s (c k)"), in_=stp_d[:, :, :]), s_st[ti["seq"]], writes=hal_p.c)

    def init_state_late(ti):
        hst, hal_r, hal_p, hal_f = states[0 if ti["prompt"] else 1]
        if ti["prompt"]:
            fw.op("pool", lambda e: e.memset(hal_f.t[:], 0.0), writes=hal_f.c, d=0.15)
        else:
            fw.dma("sp", lambda e: e.dma_start(out=hal_f.t[:].rearrange("p s c k -> p s (c k)"), in_=stf_d[:, :, :]), s_stf[ti["seq"]], writes=hal_f.c)

    def store_state_early(ti):
        hst, hal_r, hal_p, hal_f = states[0 if ti["prompt"] else 1]
        q = ti["seq"]
        q1 = q + ti["nseg"]
        fw.dma("sp", lambda e: e.dma_start(out=oh_d[:, q:q1, :], in_=hst.t[:]), s_so[q], reads=hst.c)
        fw.dma("sp", lambda e: e.dma_start(out=or_d[:, q:q1, :], in_=hal_r.t[:].rearrange("p s c k -> p s (c k)")), s_so[q], reads=hal_r.c)
        fw.dma("sp", lambda e: e.dma_start(out=op_d[:, q:q1, :], in_=hal_p.t[:].rearrange("p s c k -> p s (c k)")), s_so[q], reads=hal_p.c)

    def store_state_late(ti):
        hst, hal_r, hal_p, hal_f = states[0 if ti["prompt"] else 1]
        q = ti["seq"]
        q1 = q + ti["nseg"]
        fw.dma("sp", lambda e: e.dma_start(out=of_d[:, q:q1, :], in_=hal_f.t[:].rearrange("p s c k -> p s (c k)")), s_sof[q], reads=hal_f.c)

    def phase_B(ti):
        hst, hal_r, hal_p, hal_f = states[0 if ti["prompt"] else 1]
        T = ti["T"]
        NSG, L = ti["nseg"], ti["L"]
        cmm = 0.22 * 8 * T / 512.0 if T > 32 else 0.75
        wcur = {}
        st1 = {}
        st2 = {}

        def s1(c):
            if c % 2 == 0:
                wcur[c // 2] = ws.next(("in", c // 2))
            wh = wcur[c // 2]
            p = proj_fm(wh[1], (c % 2) * 128, xnT, T, PS)
            if c % 2 == 1:
                ws.done(wh)
            xr = S.grab()
            xr3 = seg3(xr.t[:, 0:NSG * (3 + L)], 3 + L, NSG)
            fw.op(HALO_IN_ENG, lambda e: (e.copy if HALO_IN_ENG == "act" else e.tensor_copy)(out=xr3[:, :, 0:3], in_=hal_r.t[:, :, c, :]), reads=[hal_r.c[c]], writes=[xr], d=0.2)
            fw.op("act", lambda e: e.activation(out=xr3[:, :, 3:3 + L], in_=seg3(p.t[:, 0:T], L, NSG), func=AF.Copy), reads=[p], writes=[xr])
            u = SU.grab()
            if TAP_SRC == "psum":
                fw.op("act", lambda e: e.activation(out=u.t[:, 0:T], in_=p.t[:, 0:T], func=AF.Identity, scale=vcol(V_CLW + 4 * c + 3), bias=vcol(V_CLB + c)),
                      reads=[p, vecs], writes=[u])
            else:
                fw.op("act", lambda e: e.activation(out=seg3(u.t[:, 0:T], L, NSG), in_=xr3[:, :, 3:3 + L], func=AF.Identity, scale=vcol(V_CLW + 4 * c + 3), bias=vcol(V_CLB + c)),
                      reads=[xr, vecs], writes=[u], d=0.55 * fw.scale + 0.1)
            u3 = seg3(u.t[:, 0:T], L, NSG)
            for j in range(3):
                fw.op("dve", lambda e, j=j: e.scalar_tensor_tensor(out=u3, in0=xr3[:, :, j:j + L], scalar=vcol(V_CLW + 4 * c + j), in1=u3, op0=ALU.mult, op1=ALU.add),
                      reads=[xr, u, vecs], writes=[u])
            fw.op("act", lambda e: e.copy(out=hal_r.t[:, :, c, :], in_=xr3[:, :, L:L + 3]), reads=[xr], writes=[hal_r.c[c]], d=0.2)
            ub = SB_.grab()
            if CAST_ENG == "act":
                fw.op("act", lambda e: e.activation(out=ub.t[:, 0:T], in_=u.t[:, 0:T], func=AF.Copy), reads=[u], writes=[ub])
            else:
                fw.op("dve", lambda e: e.tensor_copy(out=ub.t[:, 0:T], in_=u.t[:, 0:T]), reads=[u], writes=[ub], d=0.3 * fw.scale + 0.08)
            st1[c] = (u, ub)

        def s2a(c0):
            loc = []
            for c in (c0, c0 + 1):
                u, ub = st1.pop(c)
                pr = PS.grab()
                fw.op("pe", lambda e, pr=pr, c=c, ub=ub: e.matmul(pr.t[:, 0:T], lhsT=wra.t[:, c, :], rhs=ub.t[:, 0:T], start=True, stop=True), reads=[wra, ub], writes=[pr], d=0.25 * fw.scale + 0.1)
                pi = PS.grab()
                fw.op("pe", lambda e, pi=pi, c=c, ub=ub: e.matmul(pi.t[:, 0:T], lhsT=wix.t[:, c, :], rhs=ub.t[:, 0:T], start=True, stop=True), reads=[wix, ub], writes=[pi], d=0.25 * fw.scale + 0.1)
                A = S.grab()
                fw.op("act", lambda e, A=A, pr=pr, c=c: e.activation(out=A.t[:, 0:T], in_=pr.t[:, 0:T], func=AF.Tanh, scale=0.5, bias=dcol(DV_HBA + c)), reads=[pr, dv], writes=[A], tbl=1)
                fw.op("act", lambda e, A=A, c=c: e.activation(out=A.t[:, 0:T], in_=A.t[:, 0:T], func=AF.Exp, scale=dcol(DV_CH + c), bias=dcol(DV_CH + c)), reads=[A, dv], writes=[A], tbl=1)
                I = S.grab()
                fw.op("act", lambda e, I=I, pi=pi, c=c: e.activation(out=I.t[:, 0:T], in_=pi.t[:, 0:T], func=AF.Tanh, scale=0.5, bias=dcol(DV_HBX + c)), reads=[pi, dv], writes=[I], tbl=1)
                M = S.grab()
                fw.op("act", lambda e, M=M, A=A: e.activation(out=M.t[:, 0:T], in_=A.t[:, 0:T], func=AF.Square), reads=[A], writes=[M])
                loc.append((c, u, A, I, M))
            for (c, u, A, I, M) in loc:
                fw.op("act", lambda e, M=M: e.activation(out=M.t[:, 0:T], in_=M.t[:, 0:T], func=AF.Sqrt, scale=-1.0, bias=1.0), reads=[M], writes=[M], tbl=2)
                st2[c] = (u, A, I, M)

        def s2b(c):
            u, A, I, M = st2.pop(c)
            fw.op("dve", lambda e: e.scalar_tensor_tensor(out=I.t[:, 0:T], in0=I.t[:, 0:T], scalar=1.0, in1=u.t[:, 0:T], op0=ALU.add, op1=ALU.mult), reads=[I, u], writes=[I])
            fw.op("dve", lambda e: e.scalar_tensor_tensor(out=I.t[:, 0:T], in0=I.t[:, 0:T], scalar=0.5, in1=M.t[:, 0:T], op0=ALU.mult, op1=ALU.mult), reads=[I, M], writes=[I])
            h = M
            def scans(e):
                for g_ in range(NSG):
                    i_ = e.tensor_tensor_scan(out=h.t[:, g_ * L:(g_ + 1) * L], data0=A.t[:, g_ * L:(g_ + 1) * L], data1=I.t[:, g_ * L:(g_ + 1) * L],
                                              initial=hst.t[:, g_, c:c + 1], op0=ALU.mult, op1=ALU.add)
                return i_
            fw.op("dve", scans, reads=[A, I, hst.c[c]], writes=[h], d=1.1 * fw.scale + 0.1 * NSG)
            fw.op("act", lambda e: e.copy(out=hst.t[:, :, c:c + 1], in_=seg3(h.t[:, 0:T], L, NSG)[:, :, L - 1:L]), reads=[h], writes=[hst.c[c]], d=0.2)
            if CAST_ENG == "act":
                fw.op("act", lambda e: e.activation(out=hT.t[:, c, 0:T], in_=h.t[:, 0:T], func=AF.Copy), reads=[h], writes=[hT.c[c]])
            else:
                fw.op("dve", lambda e: e.tensor_copy(out=hT.t[:, c, 0:T], in_=h.t[:, 0:T]), reads=[h], writes=[hT.c[c]], d=0.3 * fw.scale + 0.08)

        s1(0)
        yield cmm
        s1(1)
        yield cmm
        for pr_ in range(4):
            c0 = 2 * pr_
            if c0 + 2 < KC:
                s1(c0 + 2)
                yield cmm
                s1(c0 + 3)
                yield cmm
            s2a(c0)
            yield 0.9
            s2b(c0)
            s2b(c0 + 1)
            yield 0.1

    def phase_C(ti):
        hst, hal_r, hal_p, hal_f = states[0 if ti["prompt"] else 1]
        T = ti["T"]
        NSG, LS = ti["nseg"], ti["L"]
        L = 15 + LS
        cmm = 0.22 * 8 * T / 512.0 if T > 32 else 0.75
        pend = {}

        def s1(g, n, wh, pl):
            c = 2 * g + n
            p = proj_fm(wh[1], n * 128, xnT, T, PS)
            xp = S.grab()
            xp3 = seg3(xp.t[:, 0:NSG * L], L, NSG)
            fw.op(HALO_IN_ENG, lambda e: (e.copy if HALO_IN_ENG == "act" else e.tensor_copy)(out=xp3[:, :, 0:15], in_=hal_p.t[:, :, c, :]), reads=[hal_p.c[c]], writes=[xp], d=0.2)
            fw.op("act", lambda e: e.activation(out=xp3[:, :, 15:L], in_=seg3(p.t[:, 0:T], LS, NSG), func=AF.Copy), reads=[p], writes=[xp])
            fw.op("act", lambda e: e.copy(out=hal_p.t[:, :, c, :], in_=xp3[:, :, LS:LS + 15]), reads=[xp], writes=[hal_p.c[c]], d=0.2)
            cur = xp
            for lv in range(1, g + 2):
                d = 1 << (lv - 1)
                lo = (1 << lv) - 1
                nxt = S.grab()

                def lvl(e, nxt=nxt, cur=cur, lo=lo, d=d):
                    n3 = seg3(nxt.t[:, 0:NSG * L], L, NSG)
                    c3 = seg3(cur.t[:, 0:NSG * L], L, NSG)
                    return e.tensor_tensor(out=n3[:, :, lo:L], in0=c3[:, :, lo:L], in1=c3[:, :, lo - d:L - d], op=ALU.add)
                fw.op("pool", lvl, reads=[cur], writes=[nxt], d=1.05 * fw.scale + 0.1)
                cur = nxt
            w = 1 << (g + 1)
            if ti["prompt"] and ti["first"]:
                fw.op("dve", lambda e: e.tensor_tensor(out=cur.t[:, 15:15 + w - 1], in0=cur.t[:, 15:15 + w - 1], in1=corr.t[:, 16 * g:16 * g + w - 1], op=ALU.mult),
                      reads=[cur, corr], writes=[cur])
            fw.op("dve", lambda e: e.scalar_tensor_tensor(out=seg3(pl.t[:, n, 0:T], LS, NSG), in0=seg3(cur.t[:, 0:NSG * L], L, NSG)[:, :, 15:L], scalar=1.0 / w,
                                                          in1=xp3[:, :, 15:L], op0=ALU.mult, op1=ALU.subtract),
                  reads=[cur, xp], writes=[pl])

        def s2(g):
            pl = pend.pop(g)
            for n in range(2):
                c = 2 * g + n
                p = PS.grab()

                def mm(e, p=p, n=n):
                    for kc in range(2):
                        i = e.matmul(p.t[:, 0:T], lhsT=wpl.t[:, g, kc, n * 128:(n + 1) * 128], rhs=pl.t[:, kc, 0:T], start=(kc == 0), stop=(kc == 1))
                    return i
                fw.op("pe", mm, reads=[wpl, pl], writes=[p], d=0.45 * fw.scale + 0.1)
                fw.op("act", lambda e, p=p, c=c: e.activation(out=ppT.t[:, c, 0:T], in_=p.t[:, 0:T], func=AF.Identity, scale=vcol(V_PSC + c)), reads=[p, vecs], writes=[ppT.c[c]])

        for g in range(5):
            if g < 4:
                wh = ws.next(("in", 4 + g))
                pl = plb.grab()
                for n in range(2):
                    s1(g, n, wh, pl)
                    yield cmm
                ws.done(wh)
                pend[g] = pl
            if g >= 1:
                s2(g - 1)
                yield 0.9

    def phase_D(ti):
        T = ti["T"]
        cmm = 0.22 * 16 * T / 512.0 if T > 32 else 1.5
        for j in range(4):
            tAs = []
            wA = ws.next(("brl", j))
            wgA = ws.next(("in", 8 + j))
            for mm_ in range(2):
                cs = mm_ * 128
                pg = proj_fm(wgA[1], cs, xnT, T, PS)
                pa = proj_fm(wA[1], cs, hT, T, PS)
                tA = S.grab()
                fw.op("act", lambda e, tA=tA, pg=pg: e.activation(out=tA.t[:, 0:T], in_=pg.t[:, 0:T], func=AF.Tanh, scale=0.5), reads=[pg], writes=[tA], tbl=1)
                fw.op("dve", lambda e, tA=tA, pa=pa: e.scalar_tensor_tensor(out=tA.t[:, 0:T], in0=tA.t[:, 0:T], scalar=1.0, in1=pa.t[:, 0:T], op0=ALU.add, op1=ALU.mult), reads=[tA, pa], writes=[tA])
                tAs.append(tA)
                yield cmm
            ws.done(wA)
            ws.done(wgA)
            wB = ws.next(("brp", j))
            wgB = ws.next(("in", 12 + j))
            for mm_ in range(2):
                m = 2 * j + mm_
                cs = mm_ * 128
                pg = proj_fm(wgB[1], cs, xnT, T, PS)
                pb_ = proj_fm(wB[1], cs, ppT, T, PS)
                tB = S.grab()
                tA = tAs[mm_]
                fw.op("act", lambda e, tB=tB, pg=pg: e.activation(out=tB.t[:, 0:T], in_=pg.t[:, 0:T], func=AF.Tanh, scale=0.5), reads=[pg], writes=[tB], tbl=1)
                fw.op("dve", lambda e, tB=tB, pb_=pb_: e.scalar_tensor_tensor(out=tB.t[:, 0:T], in0=tB.t[:, 0:T], scalar=1.0, in1=pb_.t[:, 0:T], op0=ALU.add, op1=ALU.mult), reads=[tB, pb_], writes=[tB])
                fw.op("dve", lambda e, tA=tA, tB=tB, m=m: e.tensor_tensor(out=mgT.t[:, m, 0:T], in0=tA.t[:, 0:T], in1=tB.t[:, 0:T], op=ALU.add), reads=[tA, tB], writes=[mgT.c[m]])
                yield cmm
            ws.done(wB)
            ws.done(wgB)

    def phase_E(ti, par):
        def epi(b, bs, h, p):
            xb = xres[par][b]
            fw.op("dve", lambda e: e.scalar_tensor_tensor(out=xb.t[0:bs, 512 * h:512 * h + 512], in0=p.t[0:bs, 0:512], scalar=0.5, in1=xb.t[0:bs, 512 * h:512 * h + 512], op0=ALU.mult, op1=ALU.add),
                  reads=[p, xb], writes=[xb])
        yield from tokproj(ti, par, mgT, 8, "out", epi, PS)

    def phase_G(ti):
        hst, hal_r, hal_p, hal_f = states[0 if ti["prompt"] else 1]
        T = ti["T"]
        NSG, L = ti["nseg"], ti["L"]
        cmm = 0.22 * 16 * T / 512.0 if T > 32 else 1.5
        pend = []

        def fin(j, cg, cvv):
            fw.op("act", lambda e: e.activation(out=cg.t[:, 0:T], in_=cg.t[:, 0:T], func=AF.Gelu_apprx_tanh), reads=[cg], writes=[cg], tbl=3)
            fw.op("dve", lambda e: e.tensor_tensor(out=actT.t[:, j, 0:T], in0=cg.t[:, 0:T], in1=cvv.t[:, 0:T], op=ALU.mult), reads=[cg, cvv], writes=[actT.c[j]])

        for jp in range(12):
            wg = ws.next(("up", jp))
            wv = ws.next(("up", 12 + jp))
            for mm_ in range(2):
                j = 2 * jp + mm_
                cs = mm_ * 128
                res = []
                for (wh, ch) in ((wg, j), (wv, 24 + j)):
                    p = proj_fm(wh[1], cs, x1nT, T, PSL)
                    hb = SL.grab()
                    hb3 = seg3(hb.t[:, 0:NSG * (2 + L)], 2 + L, NSG)
                    fw.op(HALO_IN_ENG, lambda e, hb3=hb3, ch=ch: (e.copy if HALO_IN_ENG == "act" else e.tensor_copy)(out=hb3[:, :, 0:2], in_=hal_f.t[:, :, ch, :]), reads=[hal_f.c[ch]], writes=[hb], d=0.2)
                    fw.op("act", lambda e, hb3=hb3, p=p: e.activation(out=hb3[:, :, 2:2 + L], in_=seg3(p.t[:, 0:T], L, NSG), func=AF.Copy), reads=[p], writes=[hb])
                    cv = SL.grab()
                    if (ch >= 24 and FFN_TAP_V == "dve") or (ch < 24 and FFN_TAP_G == "dve"):
                        fw.op("dve", lambda e, cv=cv, hb3=hb3, ch=ch: e.tensor_scalar(out=seg3(cv.t[:, 0:T], L, NSG), in0=hb3[:, :, 2:2 + L], scalar1=vcol(V_CFW + 3 * ch + 2), scalar2=vcol(V_CFB + ch), op0=ALU.mult, op1=ALU.add),
                              reads=[hb, vecs], writes=[cv], d=0.4 * fw.scale + 0.08)
                    else:
                        if TAP_SRC == "psum":
                            fw.op("act", lambda e, cv=cv, p=p, ch=ch: e.activation(out=cv.t[:, 0:T], in_=p.t[:, 0:T], func=AF.Identity, scale=vcol(V_CFW + 3 * ch + 2), bias=vcol(V_CFB + ch)),
                                  reads=[p, vecs], writes=[cv])
                        else:
                            fw.op("act", lambda e, cv=cv, hb3=hb3, ch=ch: e.activation(out=seg3(cv.t[:, 0:T], L, NSG), in_=hb3[:, :, 2:2 + L], func=AF.Identity, scale=vcol(V_CFW + 3 * ch + 2), bias=vcol(V_CFB + ch)),
                                  reads=[hb, vecs], writes=[cv], d=0.55 * fw.scale + 0.1)
                    for k in range(2):
                        fw.op("dve", lambda e, cv=cv, hb3=hb3, ch=ch, k=k: e.scalar_tensor_tensor(out=seg3(cv.t[:, 0:T], L, NSG), in0=hb3[:, :, k:k + L], scalar=vcol(V_CFW + 3 * ch + k),
                                                                                                in1=seg3(cv.t[:, 0:T], L, NSG), op0=ALU.mult, op1=ALU.add),
                              reads=[hb, cv, vecs], writes=[cv])
                    fw.op("act", lambda e, hb3=hb3, ch=ch: e.copy(out=hal_f.t[:, :, ch, :], in_=hb3[:, :, L:L + 2]), reads=[hb], writes=[hal_f.c[ch]], d=0.2)
                    res.append(cv)
                pend.append((j, res[0], res[1]))
                if len(pend) > 1:
                    fin(*pend.pop(0))
                yield cmm
            ws.done(wg)
            ws.done(wv)
        while pend:
            fin(*pend.pop(0))
        yield 0.1

    def phase_H(ti, par):
        blks = blocks_of(ti["T"])

        def epi(b, bs, h, p):
            xb = xres[par][b]
            fw.op("dve", lambda e: e.tensor_tensor(out=xb.t[0:bs, 512 * h:512 * h + 512], in0=p.t[0:bs, 0:512], in1=xb.t[0:bs, 512 * h:512 * h + 512], op=ALU.add),
                  reads=[p, xb], writes=[xb])
            if h == 1:
                s = norm_p1(xb, bs, statL)
                fw.op("dve", lambda e: e.scalar_tensor_tensor(out=xb.t[0:bs, :], in0=xb.t[0:bs, :], scalar=s.t[0:bs, 1:2], in1=gf.t[0:bs, :], op0=ALU.mult, op1=ALU.mult),
                      reads=[xb, s, gf], writes=[xb], d=2.1)
                c0 = blks[b][0]
                dst = ti["yd"][ti["r0"] + c0: ti["r0"] + c0 + bs, :]
                fw.dma("sp", lambda e: e.dma_start(out=dst, in_=xb.t[0:bs, :]), s_x[par][b], reads=[xb], nbytes=bs * D * 4)
        yield from tokproj(ti, par, actT, 24, "down", epi, PSL)

    xt_cnt = [0]

    def front(i):
        ti = tiles[i]
        for b, (c0, bs) in enumerate(blocks_of(ti["T"])):
            slot = xt_cnt[0] % NXT
            xt_cnt[0] += 1
            xb = xtmp[slot]
            src = ti["xd"][ti["r0"] + c0: ti["r0"] + c0 + bs, :]
            fw.dma("sp", lambda e, xb=xb, src=src, bs=bs: e.dma_start(out=xb.t[0:bs, :], in_=src), s_xt[slot], writes=[xb], nbytes=bs * D * 4)
            s = norm_p1(xb, bs, stat)
            norm_p2(xb, bs, s, V_G1, xnT, c0)
            yield 1.0
        if ti["first"]:
            init_state_early(ti)
        for v in phase_B(ti):
            yield (2.0 * v) if ti["T"] > 32 else v

    def mid(i):
        ti = tiles[i]
        par = i % 2
        blks = blocks_of(ti["T"])
        yield from phase_C(ti)
        if ti["last"]:
            store_state_early(ti)
        yield from phase_D(ti)
        load_tile(ti, par)
        yield from phase_E(ti, par)
        yield "wait_g"
        for b, (c0, bs) in enumerate(blks):
            s = norm_p1(xres[par][b], bs, stat)
            norm_p2(xres[par][b], bs, s, V_G2, x1nT, c0)
            yield 0.5

    def late(i):
        ti = tiles[i]
        par = i % 2
        if ti["first"]:
            init_state_late(ti)
        yield from phase_G(ti)
        yield "g_done"
        if ti["last"]:
            store_state_late(ti)
        yield from phase_H(ti, par)

    def scaled(gen, sc):
        while True:
            fw.scale = sc
            try:
                v = next(gen)
            except StopIteration:
                return
            yield v

    def chain(*gens):
        for g in gens:
            yield from g

    def interleave(ga, gb, key):
        tot = STAGE_TOT.get(key)
        fa, fb = (1.0 / max(tot[0], 1e-6), 1.0 / max(tot[1], 1e-6)) if tot else (1.0, 1.0)
        ta = tb_ = 0.0
        da = ga is None
        db = gb is None
        g_done = gb is None
        blocked = False
        while not (da and db):
            if blocked and g_done:
                blocked = False
            if not da and not blocked and (db or ta * fa <= tb_ * fb):
                try:
                    v = next(ga)
                    if v == "wait_g":
                        blocked = not g_done
                    else:
                        ta += v
                except StopIteration:
                    da = True
            else:
                assert not db, "deadlock in interleave"
                try:
                    v = next(gb)
                    if v == "g_done":
                        g_done = True
                    else:
                        tb_ += v
                except StopIteration:
                    db = True
                    g_done = True
        STAGE_NEW[key] = (ta, tb_)

    nt = len(tiles)
    tsc = [t_["T"] / float(TT) for t_ in tiles]

    def g_front(i):
        return scaled(front(i), tsc[i]) if i < nt else iter(())

    def g_mid(i):
        return scaled(mid(i), tsc[i]) if i < nt else iter(())

    interleave(chain(g_front(0), g_mid(0), g_front(1)), None, -1)
    for i in range(nt):
        interleave(chain(g_mid(i + 1), g_front(i + 2)), scaled(late(i), tsc[i]), i)

    if worder is None:
        st.close()
        return ws.order

    fin = [xres[p][b] for p in range(2) for b in range(4)] + [tb for q in range(2) for ct_ in states[q] for tb in ct_.c]
    fw.final_wait("sp", fin)
    fw.build(st)
    st.close()
    return nc


STAGE_TOT = {}
STAGE_NEW = {}


def build_all():
    STAGE_TOT.clear()
    build_program(None)
    STAGE_TOT.update(STAGE_NEW)
    order = build_program(None)
    return build_program(order)


def _fm(v):
    v = np.asarray(v, dtype=np.float32)
    n = v.shape[-1] // 128
    lead = v.shape[:-1]
    a = v.reshape(lead + (n, 128))
    a = np.moveaxis(a, -1, 0)
    return np.ascontiguousarray(a)


_NC_CACHE = {}


def kernel(x_prompt, x_sample, state_lru_h, state_lru_conv, state_pool, state_ffn_conv,
           norm_mix, w_in, conv_lru_w, conv_lru_b, w_ra, b_ra, w_ix, b_ix, lru_lambda,
           w_pool, pool_scale, w_br_lru, w_br_pool, w_out,
           norm_ffn, w_up, conv_ffn_w, conv_ffn_b, w_down, norm_final):
    f = lambda a: np.ascontiguousarray(np.asarray(a, dtype=np.float32))
    x_prompt, x_sample = f(x_prompt), f(x_sample)
    vecs = np.zeros((128, NV), np.float32)
    vecs[:, V_G1:V_G1 + 8] = _fm(norm_mix[0])
    vecs[:, V_G2:V_G2 + 8] = _fm(norm_ffn[0])
    vecs[:, V_CLW:V_CLW + 32] = np.transpose(_fm(conv_lru_w[0]), (0, 2, 1)).reshape(128, 32)
    vecs[:, V_CLB:V_CLB + 8] = _fm(conv_lru_b[0])
    vecs[:, V_BA:V_BA + 8] = _fm(b_ra[0])
    vecs[:, V_BX:V_BX + 8] = _fm(b_ix[0])
    vecs[:, V_LAM:V_LAM + 8] = _fm(lru_lambda[0])
    vecs[:, V_PSC:V_PSC + 8] = _fm(pool_scale[0])
    vecs[:, V_CFW:V_CFW + 144] = np.transpose(_fm(conv_ffn_w[0]), (0, 2, 1)).reshape(128, 144)
    vecs[:, V_CFB:V_CFB + 48] = _fm(conv_ffn_b[0])
    gf_bc = np.ascontiguousarray(np.broadcast_to(f(norm_final)[None, :], (128, D)))
    ident = np.eye(128, dtype=np.float32)
    corr = np.ones((128, 64), np.float32)
    for g, w in enumerate((2, 4, 8, 16)):
        for t in range(16):
            corr[:, 16 * g + t] = float(w) / float(min(t + 1, w))
    shared = dict(vecs=vecs, gf_bc=gf_bc, ident=ident, corr=corr,
                  w_in=f(w_in[0]), w_ra=f(w_ra[0]), w_ix=f(w_ix[0]), w_pool=f(w_pool[0]),
                  w_br_lru=f(w_br_lru[0]), w_br_pool=f(w_br_pool[0]), w_out=f(w_out[0]),
                  w_up=f(w_up[0]), w_down=f(w_down[0]))
    in_maps = []
    for i in range(NCORES):
        sl = slice(2 * i, 2 * i + 2)
        m = dict(shared)
        m["xp"] = x_prompt[i]
        m["xs"] = x_sample[sl].reshape(2 * DEC, D)
        m["st_h"] = np.ascontiguousarray(_fm(state_lru_h[0, sl]))
        m["st_r"] = np.ascontiguousarray(np.transpose(_fm(state_lru_conv[0, sl]), (0, 1, 3, 2)).reshape(128, 2, 24))
        m["st_p"] = np.ascontiguousarray(np.transpose(_fm(state_pool[0, sl]), (0, 1, 3, 2)).reshape(128, 2, 120))
        m["st_f"] = np.ascontiguousarray(np.transpose(_fm(state_ffn_conv[0, sl]), (0, 1, 3, 2)).reshape(128, 2, 96))
        in_maps.append(m)

    if "nc" not in _NC_CACHE:
        _NC_CACHE["nc"] = build_all()
    nc = _NC_CACHE["nc"]
    res = run_bass_kernel_spmd(nc, in_maps, core_ids=list(range(NCORES)))
    R = res.results

    def unfm(a, nch, k):
        a = np.asarray(a).reshape(128, nch, k)
        return np.ascontiguousarray(np.transpose(a, (2, 1, 0)).reshape(k, nch * 128))

    y_prompt = np.stack([np.asarray(R[i]["yp"]) for i in range(NCORES)], 0).astype(np.float32)
    y_sample = np.concatenate([np.asarray(R[i]["ys"]).reshape(2, DEC, D) for i in range(NCORES)], 0).astype(np.float32)
    p_h = np.stack([unfm(R[i]["o_h"][:, 0, :], 8, 1)[0] for i in range(NCORES)], 0)[None]
    p_lru = np.stack([unfm(R[i]["o_r"][:, 0, :], 8, 3) for i in range(NCORES)], 0)[None]
    p_pool = np.stack([unfm(R[i]["o_p"][:, 0, :], 8, 15) for i in range(NCORES)], 0)[None]
    p_ffn = np.stack([unfm(R[i]["o_f"][:, 0, :], 48, 2) for i in range(NCORES)], 0)[None]
    s_h = np.stack([unfm(R[i]["o_h"][:, 1 + s, :], 8, 1)[0] for i in range(NCORES) for s in range(2)], 0)[None]
    s_lru = np.stack([unfm(R[i]["o_r"][:, 1 + s, :], 8, 3) for i in range(NCORES) for s in range(2)], 0)[None]
    s_pool = np.stack([unfm(R[i]["o_p"][:, 1 + s, :], 8, 15) for i in range(NCORES) for s in range(2)], 0)[None]
    s_ffn = np.stack([unfm(R[i]["o_f"][:, 1 + s, :], 48, 2) for i in range(NCORES) for s in range(2)], 0)[None]
    outs = (y_prompt, y_sample, p_h, p_lru, p_pool, p_ffn, s_h, s_lru, s_pool, s_ffn)
    return tuple(np.ascontiguousarray(o, dtype=np.float32) for o in outs)
```

```python
import contextlib
import numpy as np
import concourse.bass as bass
import concourse.mybir as mybir
from concourse.bass_utils import run_bass_kernel_spmd

F32 = mybir.dt.float32
BF16 = mybir.dt.bfloat16
AF = mybir.ActivationFunctionType
ALU = mybir.AluOpType

ENGS = ("pe", "act", "dve", "pool", "sp")

D = 1024
KC = 8
DIN = 4096
DFF = 3072
NFFC = 48
SEQ = 4096
TT = 512
NPT = SEQ // TT
DEC = 16
EPS = 1e-6
NCORES = 8

V_G1, V_G2, V_CLW, V_CLB, V_BA, V_BX, V_LAM, V_PSC, V_CFW, V_CFB, NV = 0, 8, 16, 48, 56, 64, 72, 80, 88, 232, 280
DV_HBA, DV_HBX, DV_CH, DV_MH, DV_TMP, NDV = 0, 8, 16, 24, 25, 40

SAME_ENGINE_SYNC = True
HALO_IN_ENG = "act"
CAST_ENG = "dve"
PRIO = "prog"
TBL_AWARE = False
TAP_SRC = "sbuf"
DROP_S2 = False
FFN_TAP_V = "act"
FFN_TAP_G = "act"
COARSE_BUFS = ()
REORDER_OFF = ()
NS_E = 9
NS_U = 6
NS_L = 8
NSB = 4
NW = 7
NPS = 4


class Buf:
    __slots__ = ("name", "w", "r", "gen")

    def __init__(self, name):
        self.name = name
        self.w = None
        self.r = []
        self.gen = 0


class TB:
    __slots__ = ("t", "b", "gen")

    def __init__(self, t, b, gen=0):
        self.t = t
        self.b = b
        self.gen = gen


class CT:
    def __init__(self, t, name, n):
        self.t = t
        if name in COARSE_BUFS:
            b = Buf(name)
            self.c = [TB(t, b) for i in range(n)]
        else:
            self.c = [TB(t, Buf(f"{name}.{i}")) for i in range(n)]


class Op:
    __slots__ = ("idx", "eng", "fn", "preds", "dur", "dsem", "lat", "tbl")


DEF_DUR = {"pe": 1.8, "act": 0.65, "dve": 0.75, "pool": 0.45, "sp": 0.08}
SYNC_LAT = 0.25


class FW:
    def __init__(self, nc):
        self.nc = nc
        self.ops = []
        self.dsem_group = {}
        self.dsem_cnt = {}
        self.dord = {}
        self.sems = {}
        self.scale = 1.0

    def dma_sem(self, key, group=False):
        assert key not in self.dsem_group
        self.dsem_group[key] = group
        self.dsem_cnt[key] = 0
        return key

    def _add(self, e, fn, reads, writes, dur, dsem, lat):
        preds = set()
        for tb in reads:
            assert tb.gen == tb.b.gen, f"stale ring buffer {tb.b.name}"
            if tb.b.w is not None:
                preds.add(tb.b.w)
        for tb in writes:
            assert tb.gen == tb.b.gen, f"stale ring buffer {tb.b.name}"
            b = tb.b
            if b.w is not None:
                preds.add(b.w)
            preds.update(b.r)
        o = Op()
        o.idx = len(self.ops)
        o.eng = e
        o.fn = fn
        o.preds = preds
        o.dur = dur
        o.dsem = dsem
        o.lat = lat
        o.tbl = None
        self.ops.append(o)
        ws = set()
        for tb in writes:
            tb.b.w = o.idx
            tb.b.r = []
            ws.add(id(tb.b))
        for tb in reads:
            if id(tb.b) not in ws:
                tb.b.r.append(o.idx)
        return o

    def op(self, e, fn, reads=(), writes=(), d=None, tbl=None):
        if d is None:
            d = max(0.12, DEF_DUR[e] * self.scale)
        self._add(e, fn, reads, writes, d, None, 0.0).tbl = tbl

    def dma(self, q, fn, dsem, reads=(), writes=(), lat=2.0, nbytes=0):
        dur = max(DEF_DUR["sp"], nbytes / 360e3) if q == "sp" else 1.0
        o = self._add(q, fn, reads, writes, dur, dsem, lat)
        self.dsem_cnt[dsem] += 1
        self.dord[o.idx] = self.dsem_cnt[dsem]

    def final_wait(self, e, tbs):
        self._add(e, None, tbs, tbs, 0.01, None, 0.0)

    def schedule(self):
        import heapq
        ops = self.ops
        N = len(ops)
        members = {}
        for o in ops:
            if o.dsem is not None and self.dsem_group[o.dsem]:
                members.setdefault(o.dsem, []).append(o.idx)
        for o in ops:
            extra = set()
            for p in o.preds:
                ds = ops[p].dsem
                if ds is not None and self.dsem_group[ds] and o.dsem != ds:
                    extra.update(members[ds])
            o.preds |= extra
            if o.dsem is not None and self.dsem_group[o.dsem]:
                o.preds = set(p for p in o.preds if ops[p].dsem != o.dsem)
        succ = [[] for _ in range(N)]
        npred = [0] * N
        for o in ops:
            npred[o.idx] = len(o.preds)
            for p in o.preds:
                succ[p].append(o.idx)
        ready_t = [0.0] * N
        fin = [0.0] * N
        if PRIO == "blevel":
            bl = [0.0] * N
            for o in reversed(ops):
                m = 0.0
                for sidx in succ[o.idx]:
                    if bl[sidx] > m:
                        m = bl[sidx]
                bl[o.idx] = m + o.dur + o.lat
            key = [(-bl[i], i) for i in range(N)]
        else:
            key = [(i, i) for i in range(N)]
        free = {e: 0.0 for e in ENGS}
        avail = {e: [] for e in ENGS}
        now = {e: [] for e in ENGS}
        order = {e: [] for e in ENGS}
        cur_tbl = [None]
        for o in ops:
            if npred[o.idx] == 0:
                heapq.heappush(avail[o.eng], (0.0, key[o.idx], o.idx))
        done = 0

        def pick_now(e):
            nw = now[e]
            if e != "act" or not TBL_AWARE or len(nw) < 2:
                return nw[0]
            best = None
            for cand in heapq.nsmallest(5, nw):
                t = ops[cand[1]].tbl
                if t is None or t == cur_tbl[0]:
                    best = cand
                    break
            return best if best is not None else nw[0]

        while done < N:
            best = None
            for e in ENGS:
                av, nw, fr = avail[e], now[e], free[e]
                while av and av[0][0] <= fr:
                    it = heapq.heappop(av)
                    heapq.heappush(nw, (it[1], it[2]))
                if nw:
                    pk = pick_now(e)
                    cand = (fr, pk[0], e, 1, pk)
                elif av:
                    cand = (av[0][0], av[0][1], e, 0, av[0])
                else:
                    continue
                if best is None or (cand[0], cand[1]) < (best[0], best[1]):
                    best = cand
            assert best is not None, "scheduler stuck (cyclic deps?)"
            start, _, e, fromnow, item = best
            if fromnow:
                now[e].remove(item)
                heapq.heapify(now[e])
                i = item[1]
            else:
                heapq.heappop(avail[e])
                i = item[2]
            o = ops[i]
            if e == "act" and o.tbl is not None:
                if cur_tbl[0] is not None and cur_tbl[0] != o.tbl:
                    start += 1.28
                cur_tbl[0] = o.tbl
            free[e] = start + o.dur
            fin[i] = start + o.dur + o.lat
            order[e].append(i)
            done += 1
            for sidx in succ[i]:
                t = fin[i] + (SYNC_LAT if ops[sidx].eng != e or o.dsem is not None else 0.05)
                if t > ready_t[sidx]:
                    ready_t[sidx] = t
                npred[sidx] -= 1
                if npred[sidx] == 0:
                    heapq.heappush(avail[ops[sidx].eng], (ready_t[sidx], key[sidx], sidx))
        self.makespan = max(fin)
        for e in REORDER_OFF:
            order[e] = sorted(order[e])
        self.order = order
        return order

    def build(self, st):
        nc = self.nc
        ops = self.ops
        order = self.schedule()
        ev = {}
        for e in ENGS:
            n = 0
            for i in order[e]:
                o = ops[i]
                if o.dsem is None:
                    if o.fn is not None:
                        n += 1
                        ev[i] = (e, n)
                    else:
                        ev[i] = None
                else:
                    k = self.dsem_cnt[o.dsem] if self.dsem_group[o.dsem] else self.dord[i]
                    ev[i] = (o.dsem, 16 * k)
        streams = {e: [] for e in ENGS}
        for e in ENGS:
            waited = {}
            for i in order[e]:
                o = ops[i]
                need = {}
                for p in o.preds:
                    pe = ev[p]
                    if pe is None:
                        continue
                    k, v = pe
                    if k == e and not SAME_ENGINE_SYNC:
                        continue
                    if o.dsem is not None and k == o.dsem and self.dsem_group[k]:
                        continue
                    if need.get(k, 0) < v:
                        need[k] = v
                wl = []
                for k, v in need.items():
                    if waited.get(k, 0) < v:
                        waited[k] = v
                        wl.append((k, v))
                streams[e].append((i, wl))
        val = {}
        ptr = {e: 0 for e in ENGS}
        progress = True
        while progress:
            progress = False
            for e in ENGS:
                while ptr[e] < len(streams[e]):
                    i, wl = streams[e][ptr[e]]
                    if all(val.get(k, 0) >= v for k, v in wl):
                        o = ops[i]
                        if o.dsem is not None:
                            val[o.dsem] = val.get(o.dsem, 0) + 16
                        elif o.fn is not None:
                            val[e] = val.get(e, 0) + 1
                        ptr[e] += 1
                        progress = True
                    else:
                        break
        if not all(ptr[e] == len(streams[e]) for e in ENGS):
            msg = []
            for e in ENGS:
                if ptr[e] < len(streams[e]):
                    i, wl = streams[e][ptr[e]]
                    msg.append(f"{e}: op#{i} pos {ptr[e]}/{len(streams[e])} waits {[(k, v, val.get(k, 0)) for k, v in wl if val.get(k, 0) < v]} preds {sorted(ops[i].preds)[-6:]}")
            raise AssertionError("semaphore deadlock in generated program: " + " | ".join(msg))
        for k in list(ENGS) + list(self.dsem_group.keys()):
            self.sems[k] = st.enter_context(nc.semaphore("s_" + str(k)))
        block = st.enter_context(nc.Block())
        sems = self.sems

        def run(eng, e):
            for i, wl in streams[e]:
                for k, v in wl:
                    eng.wait_ge(sems[k], v)
                o = ops[i]
                if o.fn is None:
                    continue
                if o.dsem is not None:
                    o.fn(eng).then_inc(sems[o.dsem], 16)
                else:
                    o.fn(eng).then_inc(sems[e], 1)

        @block.sync
        def _(eng):
            run(eng, "sp")

        @block.tensor
        def _(eng):
            run(eng, "pe")

        @block.scalar
        def _(eng):
            run(eng, "act")

        @block.vector
        def _(eng):
            run(eng, "dve")

        @block.gpsimd
        def _(eng):
            run(eng, "pool")


class Ring:
    def __init__(self, tbs):
        self.tbs = tbs
        self.i = 0

    def grab(self):
        tb = self.tbs[self.i % len(self.tbs)]
        self.i += 1
        tb.b.gen += 1
        return TB(tb.t, tb.b, tb.b.gen)


def build_program(worder=None):
    nc = bass.Bass("TRN2", target_bir_lowering=False)
    fw = FW(nc)
    st = contextlib.ExitStack()

    def din(name, shape, dt=F32):
        return nc.dram_tensor(name, list(shape), dt, kind="ExternalInput").ap()

    def dout(name, shape, dt=F32):
        return nc.dram_tensor(name, list(shape), dt, kind="ExternalOutput").ap()

    def dint(name, shape, dt):
        return nc.dram_tensor(name, list(shape), dt).ap()

    xp_d = din("xp", [SEQ, D])
    xs_d = din("xs", [2 * DEC, D])
    sth_d = din("st_h", [128, 2, 8])
    str_d = din("st_r", [128, 2, 24])
    stp_d = din("st_p", [128, 2, 120])
    stf_d = din("st_f", [128, 2, 96])
    vecs_d = din("vecs", [128, NV])
    gf_d = din("gf_bc", [128, D])
    id_d = din("ident", [128, 128])
    corr_d = din("corr", [128, 64])
    w_in_d = din("w_in", [D, DIN])
    w_ra_d = din("w_ra", [16, 64, 64])
    w_ix_d = din("w_ix", [16, 64, 64])
    w_pool_d = din("w_pool", [4, 256, 256])
    w_brl_d = din("w_br_lru", [D, D])
    w_brp_d = din("w_br_pool", [D, D])
    w_out_d = din("w_out", [D, D])
    w_up_d = din("w_up", [D, 2 * DFF])
    w_down_d = din("w_down", [DFF, D])

    wb_in = dint("wb_in", [D, DIN], BF16)
    wb_brl = dint("wb_brl", [D, D], BF16)
    wb_brp = dint("wb_brp", [D, D], BF16)
    wb_out = dint("wb_out", [D, D], BF16)
    wb_up = dint("wb_up", [D, 2 * DFF], BF16)
    wb_down = dint("wb_down", [DFF, D], BF16)

    yp_d = dout("yp", [SEQ, D])
    ys_d = dout("ys", [2 * DEC, D])
    oh_d = dout("o_h", [128, 3, 8])
    or_d = dout("o_r", [128, 3, 24])
    op_d = dout("o_p", [128, 3, 120])
    of_d = dout("o_f", [128, 3, 96])

    def sb(name, shape, dt=F32):
        t = st.enter_context(nc.sbuf_tensor("sb_" + name, list(shape), dt))
        return TB(t, Buf(name))

    def ps(name, shape):
        t = st.enter_context(nc.psum_tensor("pp_" + name, list(shape), F32))
        return TB(t, Buf(name))

    vecs = sb("vecs", [128, NV])
    dv = sb("dv", [128, NDV])
    gf = sb("gf", [128, D])
    ident = sb("ident", [128, 128])
    corr = sb("corr", [128, 64])
    wra = sb("wra", [128, 8, 128], BF16)
    wix = sb("wix", [128, 8, 128], BF16)
    wpl = sb("wpl", [128, 4, 2, 256], BF16)
    states = []
    for q in range(2):
        ns_ = 1 + q
        a0 = sb(f"hst{q}", [128, ns_, 8])
        a1 = sb(f"hal_r{q}", [128, ns_, 8, 3])
        a2 = sb(f"hal_p{q}", [128, ns_, 8, 15])
        a3 = sb(f"hal_f{q}", [128, ns_, NFFC, 2])
        states.append((CT(a0.t, f"hst{q}", 8), CT(a1.t, f"hal_r{q}", 8), CT(a2.t, f"hal_p{q}", 8), CT(a3.t, f"hal_f{q}", NFFC)))
    xres = [[sb(f"xres{p}_{b}", [128, D]) for b in range(4)] for p in range(2)]
    dgr = Ring([sb(f"dg{i}", [128, 128]) for i in range(3)])
    NXT = 2
    xtmp = [sb(f"xtmp{i}", [128, D]) for i in range(NXT)]
    junk = sb("junk", [128, D], BF16)
    stat = Ring([sb(f"stat{i}", [128, 2]) for i in range(10)])
    statL = Ring([sb(f"statL{i}", [128, 2]) for i in range(6)])
    xnT = sb("xnT", [128, KC, TT], BF16); xnT = CT(xnT.t, "xnT", KC)
    x1nT = sb("x1nT", [128, KC, TT], BF16); x1nT = CT(x1nT.t, "x1nT", KC)
    hT = sb("hT", [128, KC, TT], BF16); hT = CT(hT.t, "hT", KC)
    ppT = sb("ppT", [128, KC, TT], BF16); ppT = CT(ppT.t, "ppT", KC)
    mgT = sb("mgT", [128, KC, TT], BF16); mgT = CT(mgT.t, "mgT", KC)
    actT = sb("actT", [128, DFF // 128, TT], BF16); actT = CT(actT.t, "actT", DFF // 128)
    S = Ring([sb(f"S{i}", [128, 528]) for i in range(NS_E)])
    SU = Ring([sb(f"SU{i}", [128, TT]) for i in range(NS_U)])
    SL = Ring([sb(f"SL{i}", [128, 516]) for i in range(NS_L)])
    SB_ = Ring([sb(f"SB{i}", [128, TT], BF16) for i in range(NSB)])
    plb = Ring([sb(f"plb{i}", [128, 2, TT], BF16) for i in range(2)])
    wslots = [sb(f"wsl{i}", [128, 2048], BF16) for i in range(NW)]
    wsem = [fw.dma_sem(f"w{i}") for i in range(NW)]
    PS = Ring([ps(f"ps{i}", [128, 512]) for i in range(NPS)])
    PSL = Ring([ps(f"psl{i}", [128, 512]) for i in range(NPS)])

    s_const = fw.dma_sem("const", group=True)
    s_cast = {k: fw.dma_sem("cast_" + k, group=(k == "small")) for k in ("small",)}
    s_x = [[fw.dma_sem(f"x{p}_{b}") for b in range(4)] for p in range(2)]
    s_xt = [fw.dma_sem(f"xt{i}") for i in range(NXT)]
    s_sm = [fw.dma_sem(f"small{i}") for i in range(3)]
    s_st = [fw.dma_sem(f"stin{q}", group=True) for q in range(3)]
    s_so = [fw.dma_sem(f"stout{q}", group=True) for q in range(3)]
    s_stf = [fw.dma_sem(f"stinf{q}") for q in range(3)]
    s_sof = [fw.dma_sem(f"stoutf{q}") for q in range(3)]

    scr = {}

    for tb, d in ((vecs, vecs_d), (gf, gf_d), (ident, id_d), (corr, corr_d)):
        fw.dma("sp", lambda e, tb=tb, d=d: e.dma_start(out=tb.t[:], in_=d), s_const, writes=[tb])

    cast_t = [0.0]

    def cast_piece(key, src, dst, c0, ncol):
        rows = src.shape[0]
        s2 = src[:, c0:c0 + ncol]
        d2 = dst[:, c0:c0 + ncol]
        if ncol > 1024:
            a = ncol // 1024
            s2 = s2.rearrange("k (a n) -> k a n", a=a)
            d2 = d2.rearrange("k (a n) -> k a n", a=a)
        cast_t[0] += rows * ncol * 6 / 330e3
        fw.dma("pool", lambda e: e.dma_start(out=d2, in_=s2), s_cast[key], writes=[scr[key]], lat=4.0 + cast_t[0])

    fw.op("dve", lambda e: e.memset(wra.t[:], 0.0), writes=[wra])
    fw.op("dve", lambda e: e.memset(wix.t[:], 0.0), writes=[wix])
    for (wt, wd, sk) in ((wra, w_ra_d, s_sm[0]), (wix, w_ix_d, s_sm[1])):
        for j in range(2):
            src = wd.rearrange("(c j) k d -> j k c d", j=2)[j]
            fw.dma("pool", lambda e, wt=wt, src=src, j=j: e.dma_start(out=wt.t[64 * j:64 * j + 64, :, 64 * j:64 * j + 64], in_=src),
                   sk, writes=[wt])
    fw.dma("pool", lambda e: e.dma_start(out=wpl.t[:], in_=w_pool_d.rearrange("g (kc p) n -> p g kc n", p=128)),
           s_sm[2], writes=[wpl])

    V = vecs.t
    DVt = dv.t
    fw.op("dve", lambda e: e.tensor_scalar(out=DVt[:, DV_HBA:DV_HBA + 8], in0=V[:, V_BA:V_BA + 8], scalar1=0.5, scalar2=None, op0=ALU.mult), reads=[vecs], writes=[dv])
    fw.op("dve", lambda e: e.tensor_scalar(out=DVt[:, DV_HBX:DV_HBX + 8], in0=V[:, V_BX:V_BX + 8], scalar1=0.5, scalar2=None, op0=ALU.mult), reads=[vecs], writes=[dv])
    fw.op("dve", lambda e: e.memset(DVt[:, DV_MH:DV_MH + 1], -0.5), writes=[dv])
    fw.op("act", lambda e: e.activation(out=DVt[:, DV_TMP:DV_TMP + 8], in_=V[:, V_LAM:V_LAM + 8], func=AF.Exp, scale=-1.0), reads=[vecs], writes=[dv])
    fw.op("act", lambda e: e.activation(out=DVt[:, DV_TMP:DV_TMP + 8], in_=DVt[:, DV_TMP:DV_TMP + 8], func=AF.Ln, bias=1.0, scale=1.0), reads=[dv], writes=[dv])
    fw.op("dve", lambda e: e.tensor_scalar(out=DVt[:, DV_CH:DV_CH + 8], in0=DVt[:, DV_TMP:DV_TMP + 8], scalar1=-4.0, scalar2=None, op0=ALU.mult), reads=[dv], writes=[dv])

    def vcol(c):
        return V[:, c:c + 1]

    def dcol(c):
        return DVt[:, c:c + 1]

    def kblock(mat, c0, nb):
        return mat.rearrange("(kc p) n -> p kc n", p=128)[:, :, c0:c0 + nb]

    def wdesc(key):
        kind = key[0]
        if kind in ("in", "brl", "brp", "up"):
            mb, mf = {"in": (wb_in, w_in_d), "brl": (wb_brl, w_brl_d), "brp": (wb_brp, w_brp_d), "up": (wb_up, w_up_d)}[kind]
            return (kblock(mb, 256 * key[1], 256), kblock(mf, 256 * key[1], 256), 8, 256)
        mb, mf = (wb_out, w_out_d) if kind == "out" else (wb_down, w_down_d)
        h, kg = key[1], key[2]
        sl = (slice(None), slice(4 * kg, 4 * kg + 4), slice(512 * h, 512 * h + 512))
        return (mb.rearrange("(kc p) n -> p kc n", p=128)[sl], mf.rearrange("(kc p) n -> p kc n", p=128)[sl], 4, 512)

    class WStream:
        def __init__(self, order):
            self.record = order is None
            self.order = [] if order is None else order
            self.issued = 0
            self.consumed = 0
            self.free = list(range(NW))
            self.loaded = {}

        def pump(self):
            if self.record:
                return
            while self.free and self.issued < len(self.order):
                key = self.order[self.issued]
                ap_b, ap_f, nk, nb = wdesc(key)
                slot = self.free.pop(0)
                tbs = wslots[slot]
                tbs.b.gen += 1
                tb = TB(tbs.t, tbs.b, tbs.b.gen)
                dst = tb.t[:, 0:nk * nb].rearrange("p (k n) -> p k n", n=nb)
                nby = 128 * nk * nb * 2
                if key not in scrb:
                    scrb[key] = TB(None, Buf("scr_" + str(key)))
                    fw.dma("pool", lambda e, dst=dst, ap_f=ap_f: e.dma_start(out=dst, in_=ap_f), s_wsw[slot], writes=[tb], lat=2.0 + 3 * nby / 360e3)
                    fw.dma("sp", lambda e, dst=dst, ap_b=ap_b: e.dma_start(out=ap_b, in_=dst), s_wb[slot], reads=[tb], writes=[scrb[key]], nbytes=nby)
                else:
                    fw.dma("sp", lambda e, dst=dst, ap_b=ap_b: e.dma_start(out=dst, in_=ap_b), wsem[slot], reads=[scrb[key]], writes=[tb], nbytes=nby)
                self.loaded[self.issued] = (slot, tb)
                self.issued += 1

        def next(self, key):
            if self.record:
                self.order.append(key)
                tbs = wslots[0]
                return (0, TB(tbs.t, tbs.b, tbs.b.gen))
            self.pump()
            assert self.order[self.consumed] == key, (self.order[self.consumed], key)
            slot, tb = self.loaded.pop(self.consumed)
            self.consumed += 1
            return (slot, tb)

        def done(self, h):
            if self.record:
                return
            self.free.append(h[0])
            self.pump()

    scrb = {}
    s_wb = [fw.dma_sem(f"wb{i}") for i in range(NW)]
    s_wsw = [fw.dma_sem(f"wsw{i}") for i in range(NW)]
    ws = WStream(worder)

    ptiles = [dict(seq=0, T=TT, first=(k == 0), last=(k == NPT - 1), xd=xp_d, yd=yp_d, r0=k * TT, prompt=True) for k in range(NPT)]
    for t_ in ptiles:
        t_["nseg"], t_["L"] = 1, TT
    stiles = [dict(seq=1, T=2 * DEC, first=True, last=True, xd=xs_d, yd=ys_d, r0=0, prompt=False, nseg=2, L=DEC)]
    tiles = ptiles[0:8] + [stiles[0]] + ptiles[8:]

    def seg3(ap2d, width, nseg):
        return ap2d.rearrange("p (s w) -> p s w", w=width)

    def blocks_of(T):
        return [(b * 128, min(128, T - b * 128)) for b in range((T + 127) // 128)]

    def norm_p1(xb, bs, ring):
        s = ring.grab()
        fw.op("act", lambda e: e.activation(out=junk.t[0:bs, :], in_=xb.t[0:bs, :], func=AF.Square, accum_out=s.t[0:bs, 0:1]),
              reads=[xb], writes=[junk, s], d=1.1)
        fw.op("dve", lambda e: e.tensor_scalar(out=s.t[0:bs, 0:1], in0=s.t[0:bs, 0:1], scalar1=1.0 / D, scalar2=EPS, op0=ALU.mult, op1=ALU.add),
              reads=[s], writes=[s], d=0.15)
        fw.op("pool", lambda e: e.tensor_tensor(out=s.t[0:bs, 1:2], in0=s.t[0:bs, 0:1], in1=DVt[0:bs, DV_MH:DV_MH + 1], op=ALU.pow),
              reads=[s, dv], writes=[s], d=0.5)
        return s

    def norm_p2(xb, bs, s, gcol, dstT, col0):
        dg = dgr.grab()
        fw.op("dve", lambda e: e.tensor_scalar(out=dg.t[0:bs, 0:bs], in0=ident.t[0:bs, 0:bs], scalar1=s.t[0:bs, 1:2], scalar2=None, op0=ALU.mult),
              reads=[s, ident], writes=[dg], d=0.15)
        pa = PS.grab()
        pb = PS.grab()

        def tr(e):
            for c in range(KC):
                pt = pa if c < 4 else pb
                i = e.matmul(pt.t[:, (c % 4) * 128:(c % 4) * 128 + bs], lhsT=xb.t[0:bs, c * 128:(c + 1) * 128], rhs=dg.t[0:bs, 0:bs], start=True, stop=True)
            return i
        fw.op("pe", tr, reads=[xb, dg], writes=[pa, pb], d=1.9 * (bs / 128.0) + 0.2)

        def ev_d(e):
            for c in range(0, 4):
                i = e.tensor_scalar(out=dstT.t[:, c, col0:col0 + bs], in0=pa.t[:, c * 128:c * 128 + bs], scalar1=vcol(gcol + c), scalar2=None, op0=ALU.mult)
            return i

        def ev_a(e):
            for c in range(4, KC):
                i = e.activation(out=dstT.t[:, c, col0:col0 + bs], in_=pb.t[:, (c - 4) * 128:(c - 4) * 128 + bs], func=AF.Identity, scale=vcol(gcol + c))
            return i
        fw.op("dve", ev_d, reads=[pa, vecs], writes=[dstT.c[c] for c in range(0, 4)], d=1.1 * fw.scale + 0.1)
        fw.op("act", ev_a, reads=[pb, vecs], writes=[dstT.c[c] for c in range(4, KC)], d=1.1 * fw.scale + 0.1)

    def load_tile(ti, par):
        T = ti["T"]
        for b, (c0, bs) in enumerate(blocks_of(T)):
            xb = xres[par][b]
            src = ti["xd"][ti["r0"] + c0: ti["r0"] + c0 + bs, :]
            fw.dma("sp", lambda e, xb=xb, src=src, bs=bs: e.dma_start(out=xb.t[0:bs, :], in_=src), s_x[par][b], writes=[xb], nbytes=bs * D * 4)

    def proj_fm(wtb, cs, srcT, T, ring):
        p = ring.grab()
        wv = wtb.t[:, 0:2048].rearrange("p (k n) -> p k n", n=256)

        def mm(e):
            for k in range(KC):
                i = e.matmul(p.t[:, 0:T], lhsT=wv[:, k, cs:cs + 128], rhs=srcT.t[:, k, 0:T], start=(k == 0), stop=(k == KC - 1))
            return i
        fw.op("pe", mm, reads=[wtb] + srcT.c, writes=[p], d=0.22 * 8 * fw.scale + 0.1)
        return p

    def tokproj(ti, par, srcT, nk, wkey, epi, ring):
        T = ti["T"]
        blks = blocks_of(T)
        cost = 0.22 * 4 * len(blks) if T > 32 else 1.5
        for h in range(2):
            pss = [ring.grab() for _ in blks]
            for kg in range(nk // 4):
                wh = ws.next((wkey, h, kg))
                wv = wh[1].t[:, 0:2048].rearrange("p (k n) -> p k n", n=512)

                def mm(e, kg=kg, wv=wv, pss=pss):
                    for kk in range(4):
                        kc = kg * 4 + kk
                        for b, (c0, bs) in enumerate(blks):
                            i = e.matmul(pss[b].t[0:bs, 0:512], lhsT=srcT.t[:, kc, c0:c0 + bs], rhs=wv[:, kk, :], start=(kc == 0), stop=(kc == nk - 1))
                    return i
                fw.op("pe", mm, reads=[wh[1]] + srcT.c[4 * kg:4 * kg + 4], writes=pss, d=0.22 * 4 * len(blks) * fw.scale + 0.1)
                ws.done(wh)
                yield cost
            for b, (c0, bs) in enumerate(blks):
                epi(b, bs, h, pss[b])
            yield 0.1

    def init_state_early(ti):
        hst, hal_r, hal_p, hal_f = states[0 if ti["prompt"] else 1]
        if ti["prompt"]:
            fw.op("pool", lambda e: e.memset(hst.t[:], 0.0), writes=hst.c, d=0.1)
            fw.op("pool", lambda e: e.memset(hal_r.t[:], 0.0), writes=hal_r.c, d=0.1)
            fw.op("pool", lambda e: e.memset(hal_p.t[:], 0.0), writes=hal_p.c, d=0.15)
        else:
            fw.dma("sp", lambda e: e.dma_start(out=hst.t[:], in_=sth_d[:, :, :]), s_st[ti["seq"]], writes=hst.c)
            fw.dma("sp", lambda e: e.dma_start(out=hal_r.t[:].rearrange("p s c k -> p s (c k)"), in_=str_d[:, :, :]), s_st[ti["seq"]], writes=hal_r.c)
            fw.dma("sp", lambda e: e.dma_start(out=hal_p.t[:].rearrange("p s c k -> p s (c k)"), in_=stp_d[:, :, :]), s_st[ti["seq"]], writes=hal_p.c)

    def init_state_late(ti):
        hst, hal_r, hal_p, hal_f = states[0 if ti["prompt"] else 1]
        if ti["prompt"]:
            fw.op("pool", lambda e: e.memset(hal_f.t[:], 0.0), writes=hal_f.c, d=0.15)
        else:
            fw.dma("sp", lambda e: e.dma_start(out=hal_f.t[:].rearrange("p s c k -> p s (c k)"), in_=stf_d[:, :, :]), s_stf[ti["seq"]], writes=hal_f.c)

    def store_state_early(ti):
        hst, hal_r, hal_p, hal_f = states[0 if ti["prompt"] else 1]
        q = ti["seq"]
        q1 = q + ti["nseg"]
        fw.dma("sp", lambda e: e.dma_start(out=oh_d[:, q:q1, :], in_=hst.t[:]), s_so[q], reads=hst.c)
        fw.dma("sp", lambda e: e.dma_start(out=or_d[:, q:q1, :], in_=hal_r.t[:].rearrange("p s c k -> p s (c k)")), s_so[q], reads=hal_r.c)
        fw.dma("sp", lambda e: e.dma_start(out=op_d[:, q:q1, :], in_=hal_p.t[:].rearrange("p s c k -> p s (c k)")), s_so[q], reads=hal_p.c)

    def store_state_late(ti):
        hst, hal_r, hal_p, hal_f = states[0 if ti["prompt"] else 1]
        q = ti["seq"]
        q1 = q + ti["nseg"]
        fw.dma("sp", lambda e: e.dma_start(out=of_d[:, q:q1, :], in_=hal_f.t[:].rearrange("p s c k -> p s (c k)")), s_sof[q], reads=hal_f.c)

    def phase_B(ti):
        hst, hal_r, hal_p, hal_f = states[0 if ti["prompt"] else 1]
        T = ti["T"]
        NSG, L = ti["nseg"], ti["L"]
        cmm = 0.22 * 8 if T > 32 else 0.75
        wcur = {}
        st1 = {}
        st2 = {}

        def s1(c):
            if c % 2 == 0:
                wcur[c // 2] = ws.next(("in", c // 2))
            wh = wcur[c // 2]
            p = proj_fm(wh[1], (c % 2) * 128, xnT, T, PS)
            if c % 2 == 1:
                ws.done(wh)
            xr = S.grab()
            xr3 = seg3(xr.t[:, 0:NSG * (3 + L)], 3 + L, NSG)
            fw.op(HALO_IN_ENG, lambda e: (e.copy if HALO_IN_ENG == "act" else e.tensor_copy)(out=xr3[:, :, 0:3], in_=hal_r.t[:, :, c, :]), reads=[hal_r.c[c]], writes=[xr], d=0.2)
            fw.op("act", lambda e: e.activation(out=xr3[:, :, 3:3 + L], in_=seg3(p.t[:, 0:T], L, NSG), func=AF.Copy), reads=[p], writes=[xr])
            u = SU.grab()
            if TAP_SRC == "psum":
                fw.op("act", lambda e: e.activation(out=u.t[:, 0:T], in_=p.t[:, 0:T], func=AF.Identity, scale=vcol(V_CLW + 4 * c + 3), bias=vcol(V_CLB + c)),
                      reads=[p, vecs], writes=[u])
            else:
                fw.op("act", lambda e: e.activation(out=seg3(u.t[:, 0:T], L, NSG), in_=xr3[:, :, 3:3 + L], func=AF.Identity, scale=vcol(V_CLW + 4 * c + 3), bias=vcol(V_CLB + c)),
                      reads=[xr, vecs], writes=[u], d=0.55 * fw.scale + 0.1)
            u3 = seg3(u.t[:, 0:T], L, NSG)
            for j in range(3):
                fw.op("dve", lambda e, j=j: e.scalar_tensor_tensor(out=u3, in0=xr3[:, :, j:j + L], scalar=vcol(V_CLW + 4 * c + j), in1=u3, op0=ALU.mult, op1=ALU.add),
                      reads=[xr, u, vecs], writes=[u])
            fw.op("act", lambda e: e.copy(out=hal_r.t[:, :, c, :], in_=xr3[:, :, L:L + 3]), reads=[xr], writes=[hal_r.c[c]], d=0.2)
            ub = SB_.grab()
            if CAST_ENG == "act":
                fw.op("act", lambda e: e.activation(out=ub.t[:, 0:T], in_=u.t[:, 0:T], func=AF.Copy), reads=[u], writes=[ub])
            else:
                fw.op("dve", lambda e: e.tensor_copy(out=ub.t[:, 0:T], in_=u.t[:, 0:T]), reads=[u], writes=[ub], d=0.3 * fw.scale + 0.08)
            st1[c] = (u, ub)

        def s2a(c0):
            loc = []
            for c in (c0, c0 + 1):
                u, ub = st1.pop(c)
                pr = PS.grab()
                fw.op("pe", lambda e, pr=pr, c=c, ub=ub: e.matmul(pr.t[:, 0:T], lhsT=wra.t[:, c, :], rhs=ub.t[:, 0:T], start=True, stop=True), reads=[wra, ub], writes=[pr], d=0.25 * fw.scale + 0.1)
                pi = PS.grab()
                fw.op("pe", lambda e, pi=pi, c=c, ub=ub: e.matmul(pi.t[:, 0:T], lhsT=wix.t[:, c, :], rhs=ub.t[:, 0:T], start=True, stop=True), reads=[wix, ub], writes=[pi], d=0.25 * fw.scale + 0.1)
                A = S.grab()
                fw.op("act", lambda e, A=A, pr=pr, c=c: e.activation(out=A.t[:, 0:T], in_=pr.t[:, 0:T], func=AF.Tanh, scale=0.5, bias=dcol(DV_HBA + c)), reads=[pr, dv], writes=[A], tbl=1)
                fw.op("act", lambda e, A=A, c=c: e.activation(out=A.t[:, 0:T], in_=A.t[:, 0:T], func=AF.Exp, scale=dcol(DV_CH + c), bias=dcol(DV_CH + c)), reads=[A, dv], writes=[A], tbl=1)
                I = S.grab()
                fw.op("act", lambda e, I=I, pi=pi, c=c: e.activation(out=I.t[:, 0:T], in_=pi.t[:, 0:T], func=AF.Tanh, scale=0.5, bias=dcol(DV_HBX + c)), reads=[pi, dv], writes=[I], tbl=1)
                M = S.grab()
                fw.op("act", lambda e, M=M, A=A: e.activation(out=M.t[:, 0:T], in_=A.t[:, 0:T], func=AF.Square), reads=[A], writes=[M])
                loc.append((c, u, A, I, M))
            for (c, u, A, I, M) in loc:
                fw.op("act", lambda e, M=M: e.activation(out=M.t[:, 0:T], in_=M.t[:, 0:T], func=AF.Sqrt, scale=-1.0, bias=1.0), reads=[M], writes=[M], tbl=2)
                st2[c] = (u, A, I, M)

        def s2b(c):
            u, A, I, M = st2.pop(c)
            fw.op("dve", lambda e: e.scalar_tensor_tensor(out=I.t[:, 0:T], in0=I.t[:, 0:T], scalar=1.0, in1=u.t[:, 0:T], op0=ALU.add, op1=ALU.mult), reads=[I, u], writes=[I])
            fw.op("dve", lambda e: e.scalar_tensor_tensor(out=I.t[:, 0:T], in0=I.t[:, 0:T], scalar=0.5, in1=M.t[:, 0:T], op0=ALU.mult, op1=ALU.mult), reads=[I, M], writes=[I])
            h = M
            def scans(e):
                for g_ in range(NSG):
                    i_ = e.tensor_tensor_scan(out=h.t[:, g_ * L:(g_ + 1) * L], data0=A.t[:, g_ * L:(g_ + 1) * L], data1=I.t[:, g_ * L:(g_ + 1) * L],
                                              initial=hst.t[:, g_, c:c + 1], op0=ALU.mult, op1=ALU.add)
                return i_
            fw.op("dve", scans, reads=[A, I, hst.c[c]], writes=[h], d=1.1 * fw.scale + 0.1 * NSG)
            fw.op("act", lambda e: e.copy(out=hst.t[:, :, c:c + 1], in_=seg3(h.t[:, 0:T], L, NSG)[:, :, L - 1:L]), reads=[h], writes=[hst.c[c]], d=0.2)
            if CAST_ENG == "act":
                fw.op("act", lambda e: e.activation(out=hT.t[:, c, 0:T], in_=h.t[:, 0:T], func=AF.Copy), reads=[h], writes=[hT.c[c]])
            else:
                fw.op("dve", lambda e: e.tensor_copy(out=hT.t[:, c, 0:T], in_=h.t[:, 0:T]), reads=[h], writes=[hT.c[c]], d=0.3 * fw.scale + 0.08)

        s1(0)
        yield cmm
        s1(1)
        yield cmm
        for pr_ in range(4):
            c0 = 2 * pr_
            if c0 + 2 < KC:
                s1(c0 + 2)
                yield cmm
                s1(c0 + 3)
                yield cmm
            s2a(c0)
            yield 0.9
            s2b(c0)
            s2b(c0 + 1)
            yield 0.1

    def phase_C(ti):
        hst, hal_r, hal_p, hal_f = states[0 if ti["prompt"] else 1]
        T = ti["T"]
        NSG, LS = ti["nseg"], ti["L"]
        L = 15 + LS
        cmm = 0.22 * 8 if T > 32 else 0.75
        pend = {}

        def s1(g, n, wh, pl):
            c = 2 * g + n
            p = proj_fm(wh[1], n * 128, xnT, T, PS)
            xp = S.grab()
            xp3 = seg3(xp.t[:, 0:NSG * L], L, NSG)
            fw.op(HALO_IN_ENG, lambda e: (e.copy if HALO_IN_ENG == "act" else e.tensor_copy)(out=xp3[:, :, 0:15], in_=hal_p.t[:, :, c, :]), reads=[hal_p.c[c]], writes=[xp], d=0.2)
            fw.op("act", lambda e: e.activation(out=xp3[:, :, 15:L], in_=seg3(p.t[:, 0:T], LS, NSG), func=AF.Copy), reads=[p], writes=[xp])
            fw.op("act", lambda e: e.copy(out=hal_p.t[:, :, c, :], in_=xp3[:, :, LS:LS + 15]), reads=[xp], writes=[hal_p.c[c]], d=0.2)
            cur = xp
            for lv in range(1, g + 2):
                d = 1 << (lv - 1)
                lo = (1 << lv) - 1
                nxt = S.grab()

                def lvl(e, nxt=nxt, cur=cur, lo=lo, d=d):
                    n3 = seg3(nxt.t[:, 0:NSG * L], L, NSG)
                    c3 = seg3(cur.t[:, 0:NSG * L], L, NSG)
                    return e.tensor_tensor(out=n3[:, :, lo:L], in0=c3[:, :, lo:L], in1=c3[:, :, lo - d:L - d], op=ALU.add)
                fw.op("pool", lvl, reads=[cur], writes=[nxt], d=1.05 * fw.scale + 0.1)
                cur = nxt
            w = 1 << (g + 1)
            if ti["prompt"] and ti["first"]:
                fw.op("dve", lambda e: e.tensor_tensor(out=cur.t[:, 15:15 + w - 1], in0=cur.t[:, 15:15 + w - 1], in1=corr.t[:, 16 * g:16 * g + w - 1], op=ALU.mult),
                      reads=[cur, corr], writes=[cur])
            fw.op("dve", lambda e: e.scalar_tensor_tensor(out=seg3(pl.t[:, n, 0:T], LS, NSG), in0=seg3(cur.t[:, 0:NSG * L], L, NSG)[:, :, 15:L], scalar=1.0 / w,
                                                          in1=xp3[:, :, 15:L], op0=ALU.mult, op1=ALU.subtract),
                  reads=[cur, xp], writes=[pl])

        def s2(g):
            pl = pend.pop(g)
            for n in range(2):
                c = 2 * g + n
                p = PS.grab()

                def mm(e, p=p, n=n):
                    for kc in range(2):
                        i = e.matmul(p.t[:, 0:T], lhsT=wpl.t[:, g, kc, n * 128:(n + 1) * 128], rhs=pl.t[:, kc, 0:T], start=(kc == 0), stop=(kc == 1))
                    return i
                fw.op("pe", mm, reads=[wpl, pl], writes=[p], d=0.45 * fw.scale + 0.1)
                fw.op("act", lambda e, p=p, c=c: e.activation(out=ppT.t[:, c, 0:T], in_=p.t[:, 0:T], func=AF.Identity, scale=vcol(V_PSC + c)), reads=[p, vecs], writes=[ppT.c[c]])

        for g in range(5):
            if g < 4:
                wh = ws.next(("in", 4 + g))
                pl = plb.grab()
                for n in range(2):
                    s1(g, n, wh, pl)
                    yield cmm
                ws.done(wh)
                pend[g] = pl
            if g >= 1:
                s2(g - 1)
                yield 0.9

    def phase_D(ti):
        T = ti["T"]
        cmm = 0.22 * 16 if T > 32 else 1.5
        for j in range(4):
            tAs = []
            wA = ws.next(("brl", j))
            wgA = ws.next(("in", 8 + j))
            for mm_ in range(2):
                cs = mm_ * 128
                pg = proj_fm(wgA[1], cs, xnT, T, PS)
                pa = proj_fm(wA[1], cs, hT, T, PS)
                tA = S.grab()
                fw.op("act", lambda e, tA=tA, pg=pg: e.activation(out=tA.t[:, 0:T], in_=pg.t[:, 0:T], func=AF.Tanh, scale=0.5), reads=[pg], writes=[tA], tbl=1)
                fw.op("dve", lambda e, tA=tA, pa=pa: e.scalar_tensor_tensor(out=tA.t[:, 0:T], in0=tA.t[:, 0:T], scalar=1.0, in1=pa.t[:, 0:T], op0=ALU.add, op1=ALU.mult), reads=[tA, pa], writes=[tA])
                tAs.append(tA)
                yield cmm
            ws.done(wA)
            ws.done(wgA)
            wB = ws.next(("brp", j))
            wgB = ws.next(("in", 12 + j))
            for mm_ in range(2):
                m = 2 * j + mm_
                cs = mm_ * 128
                pg = proj_fm(wgB[1], cs, xnT, T, PS)
                pb_ = proj_fm(wB[1], cs, ppT, T, PS)
                tB = S.grab()
                tA = tAs[mm_]
                fw.op("act", lambda e, tB=tB, pg=pg: e.activation(out=tB.t[:, 0:T], in_=pg.t[:, 0:T], func=AF.Tanh, scale=0.5), reads=[pg], writes=[tB], tbl=1)
                fw.op("dve", lambda e, tB=tB, pb_=pb_: e.scalar_tensor_tensor(out=tB.t[:, 0:T], in0=tB.t[:, 0:T], scalar=1.0, in1=pb_.t[:, 0:T], op0=ALU.add, op1=ALU.mult), reads=[tB, pb_], writes=[tB])
                fw.op("dve", lambda e, tA=tA, tB=tB, m=m: e.tensor_tensor(out=mgT.t[:, m, 0:T], in0=tA.t[:, 0:T], in1=tB.t[:, 0:T], op=ALU.add), reads=[tA, tB], writes=[mgT.c[m]])
                yield cmm
            ws.done(wB)
            ws.done(wgB)

    def phase_E(ti, par):
        def epi(b, bs, h, p):
            xb = xres[par][b]
            fw.op("dve", lambda e: e.scalar_tensor_tensor(out=xb.t[0:bs, 512 * h:512 * h + 512], in0=p.t[0:bs, 0:512], scalar=0.5, in1=xb.t[0:bs, 512 * h:512 * h + 512], op0=ALU.mult, op1=ALU.add),
                  reads=[p, xb], writes=[xb])
        yield from tokproj(ti, par, mgT, 8, "out", epi, PS)

    def phase_G(ti):
        hst, hal_r, hal_p, hal_f = states[0 if ti["prompt"] else 1]
        T = ti["T"]
        NSG, L = ti["nseg"], ti["L"]
        cmm = 0.22 * 16 if T > 32 else 1.5
        pend = []

        def fin(j, cg, cvv):
            fw.op("act", lambda e: e.activation(out=cg.t[:, 0:T], in_=cg.t[:, 0:T], func=AF.Gelu_apprx_tanh), reads=[cg], writes=[cg], tbl=3)
            fw.op("dve", lambda e: e.tensor_tensor(out=actT.t[:, j, 0:T], in0=cg.t[:, 0:T], in1=cvv.t[:, 0:T], op=ALU.mult), reads=[cg, cvv], writes=[actT.c[j]])

        for jp in range(12):
            wg = ws.next(("up", jp))
            wv = ws.next(("up", 12 + jp))
            for mm_ in range(2):
                j = 2 * jp + mm_
                cs = mm_ * 128
                res = []
                for (wh, ch) in ((wg, j), (wv, 24 + j)):
                    p = proj_fm(wh[1], cs, x1nT, T, PSL)
                    hb = SL.grab()
                    hb3 = seg3(hb.t[:, 0:NSG * (2 + L)], 2 + L, NSG)
                    fw.op(HALO_IN_ENG, lambda e, hb3=hb3, ch=ch: (e.copy if HALO_IN_ENG == "act" else e.tensor_copy)(out=hb3[:, :, 0:2], in_=hal_f.t[:, :, ch, :]), reads=[hal_f.c[ch]], writes=[hb], d=0.2)
                    fw.op("act", lambda e, hb3=hb3, p=p: e.activation(out=hb3[:, :, 2:2 + L], in_=seg3(p.t[:, 0:T], L, NSG), func=AF.Copy), reads=[p], writes=[hb])
                    cv = SL.grab()
                    if (ch >= 24 and FFN_TAP_V == "dve") or (ch < 24 and FFN_TAP_G == "dve"):
                        fw.op("dve", lambda e, cv=cv, hb3=hb3, ch=ch: e.tensor_scalar(out=seg3(cv.t[:, 0:T], L, NSG), in0=hb3[:, :, 2:2 + L], scalar1=vcol(V_CFW + 3 * ch + 2), scalar2=vcol(V_CFB + ch), op0=ALU.mult, op1=ALU.add),
                              reads=[hb, vecs], writes=[cv], d=0.4 * fw.scale + 0.08)
                    else:
                        if TAP_SRC == "psum":
                            fw.op("act", lambda e, cv=cv, p=p, ch=ch: e.activation(out=cv.t[:, 0:T], in_=p.t[:, 0:T], func=AF.Identity, scale=vcol(V_CFW + 3 * ch + 2), bias=vcol(V_CFB + ch)),
                                  reads=[p, vecs], writes=[cv])
                        else:
                            fw.op("act", lambda e, cv=cv, hb3=hb3, ch=ch: e.activation(out=seg3(cv.t[:, 0:T], L, NSG), in_=hb3[:, :, 2:2 + L], func=AF.Identity, scale=vcol(V_CFW + 3 * ch + 2), bias=vcol(V_CFB + ch)),
                                  reads=[hb, vecs], writes=[cv], d=0.55 * fw.scale + 0.1)
                    for k in range(2):
                        fw.op("dve", lambda e, cv=cv, hb3=hb3, ch=ch, k=k: e.scalar_tensor_tensor(out=seg3(cv.t[:, 0:T], L, NSG), in0=hb3[:, :, k:k + L], scalar=vcol(V_CFW + 3 * ch + k),
                                                                                                in1=seg3(cv.t[:, 0:T], L, NSG), op0=ALU.mult, op1=ALU.add),
                              reads=[hb, cv, vecs], writes=[cv])
                    fw.op("act", lambda e, hb3=hb3, ch=ch: e.copy(out=hal_f.t[:, :, ch, :], in_=hb3[:, :, L:L + 2]), reads=[hb], writes=[hal_f.c[ch]], d=0.2)
                    res.append(cv)
                pend.append((j, res[0], res[1]))
                if len(pend) > 1:
                    fin(*pend.pop(0))
                yield cmm
            ws.done(wg)
            ws.done(wv)
        while pend:
            fin(*pend.pop(0))
        yield 0.1

    def phase_H(ti, par):
        blks = blocks_of(ti["T"])

        def epi(b, bs, h, p):
            xb = xres[par][b]
            fw.op("dve", lambda e: e.tensor_tensor(out=xb.t[0:bs, 512 * h:512 * h + 512], in0=p.t[0:bs, 0:512], in1=xb.t[0:bs, 512 * h:512 * h + 512], op=ALU.add),
                  reads=[p, xb], writes=[xb])
            if h == 1:
                s = norm_p1(xb, bs, statL)
                fw.op("dve", lambda e: e.scalar_tensor_tensor(out=xb.t[0:bs, :], in0=xb.t[0:bs, :], scalar=s.t[0:bs, 1:2], in1=gf.t[0:bs, :], op0=ALU.mult, op1=ALU.mult),
                      reads=[xb, s, gf], writes=[xb], d=2.1)
                c0 = blks[b][0]
                dst = ti["yd"][ti["r0"] + c0: ti["r0"] + c0 + bs, :]
                fw.dma("sp", lambda e: e.dma_start(out=dst, in_=xb.t[0:bs, :]), s_x[par][b], reads=[xb], nbytes=bs * D * 4)
        yield from tokproj(ti, par, actT, 24, "down", epi, PSL)

    xt_cnt = [0]

    def front(i):
        ti = tiles[i]
        for b, (c0, bs) in enumerate(blocks_of(ti["T"])):
            slot = xt_cnt[0] % NXT
            xt_cnt[0] += 1
            xb = xtmp[slot]
            src = ti["xd"][ti["r0"] + c0: ti["r0"] + c0 + bs, :]
            fw.dma("sp", lambda e, xb=xb, src=src, bs=bs: e.dma_start(out=xb.t[0:bs, :], in_=src), s_xt[slot], writes=[xb], nbytes=bs * D * 4)
            s = norm_p1(xb, bs, stat)
            norm_p2(xb, bs, s, V_G1, xnT, c0)
            yield 1.0
        if ti["first"]:
            init_state_early(ti)
        for v in phase_B(ti):
            yield (2.0 * v) if ti["T"] > 32 else v

    def mid(i):
        ti = tiles[i]
        par = i % 2
        blks = blocks_of(ti["T"])
        yield from phase_C(ti)
        if ti["last"]:
            store_state_early(ti)
        yield from phase_D(ti)
        load_tile(ti, par)
        yield from phase_E(ti, par)
        yield "wait_g"
        for b, (c0, bs) in enumerate(blks):
            s = norm_p1(xres[par][b], bs, stat)
            norm_p2(xres[par][b], bs, s, V_G2, x1nT, c0)
            yield 0.5

    def late(i):
        ti = tiles[i]
        par = i % 2
        if ti["first"]:
            init_state_late(ti)
        yield from phase_G(ti)
        yield "g_done"
        if ti["last"]:
            store_state_late(ti)
        yield from phase_H(ti, par)

    def scaled(gen, sc):
        while True:
            fw.scale = sc
            try:
                v = next(gen)
            except StopIteration:
                return
            yield v

    def chain(*gens):
        for g in gens:
            yield from g

    def interleave(ga, gb, key):
        tot = STAGE_TOT.get(key)
        fa, fb = (1.0 / max(tot[0], 1e-6), 1.0 / max(tot[1], 1e-6)) if tot else (1.0, 1.0)
        ta = tb_ = 0.0
        da = ga is None
        db = gb is None
        g_done = gb is None
        blocked = False
        while not (da and db):
            if blocked and g_done:
                blocked = False
            if not da and not blocked and (db or ta * fa <= tb_ * fb):
                try:
                    v = next(ga)
                    if v == "wait_g":
                        blocked = not g_done
                    else:
                        ta += v
                except StopIteration:
                    da = True
            else:
                assert not db, "deadlock in interleave"
                try:
                    v = next(gb)
                    if v == "g_done":
                        g_done = True
                    else:
                        tb_ += v
                except StopIteration:
                    db = True
                    g_done = True
        STAGE_NEW[key] = (ta, tb_)

    nt = len(tiles)
    tsc = [t_["T"] / float(TT) for t_ in tiles]

    def g_front(i):
        return scaled(front(i), tsc[i]) if i < nt else iter(())

    def g_mid(i):
        return scaled(mid(i), tsc[i]) if i < nt else iter(())

    interleave(chain(g_front(0), g_mid(0), g_front(1)), None, -1)
    for i in range(nt):
        interleave(chain(g_mid(i + 1), g_front(i + 2)), scaled(late(i), tsc[i]), i)

    if worder is None:
        st.close()
        return ws.order

    fin = [xres[p][b] for p in range(2) for b in range(4)] + [tb for q in range(2) for ct_ in states[q] for tb in ct_.c]
    fw.final_wait("sp", fin)
    fw.build(st)
    st.close()
    return nc


STAGE_TOT = {}
STAGE_NEW = {}


def build_all():
    STAGE_TOT.clear()
    build_program(None)
    STAGE_TOT.update(STAGE_NEW)
    order = build_program(None)
    return build_program(order)


def _fm(v):
    v = np.asarray(v, dtype=np.float32)
    n = v.shape[-1] // 128
    lead = v.shape[:-1]
    a = v.reshape(lead + (n, 128))
    a = np.moveaxis(a, -1, 0)
    return np.ascontiguousarray(a)


_NC_CACHE = {}


def kernel(x_prompt, x_sample, state_lru_h, state_lru_conv, state_pool, state_ffn_conv,
           norm_mix, w_in, conv_lru_w, conv_lru_b, w_ra, b_ra, w_ix, b_ix, lru_lambda,
           w_pool, pool_scale, w_br_lru, w_br_pool, w_out,
           norm_ffn, w_up, conv_ffn_w, conv_ffn_b, w_down, norm_final):
    f = lambda a: np.ascontiguousarray(np.asarray(a, dtype=np.float32))
    x_prompt, x_sample = f(x_prompt), f(x_sample)
    vecs = np.zeros((128, NV), np.float32)
    vecs[:, V_G1:V_G1 + 8] = _fm(norm_mix[0])
    vecs[:, V_G2:V_G2 + 8] = _fm(norm_ffn[0])
    vecs[:, V_CLW:V_CLW + 32] = np.transpose(_fm(conv_lru_w[0]), (0, 2, 1)).reshape(128, 32)
    vecs[:, V_CLB:V_CLB + 8] = _fm(conv_lru_b[0])
    vecs[:, V_BA:V_BA + 8] = _fm(b_ra[0])
    vecs[:, V_BX:V_BX + 8] = _fm(b_ix[0])
    vecs[:, V_LAM:V_LAM + 8] = _fm(lru_lambda[0])
    vecs[:, V_PSC:V_PSC + 8] = _fm(pool_scale[0])
    vecs[:, V_CFW:V_CFW + 144] = np.transpose(_fm(conv_ffn_w[0]), (0, 2, 1)).reshape(128, 144)
    vecs[:, V_CFB:V_CFB + 48] = _fm(conv_ffn_b[0])
    gf_bc = np.ascontiguousarray(np.broadcast_to(f(norm_final)[None, :], (128, D)))
    ident = np.eye(128, dtype=np.float32)
    corr = np.ones((128, 64), np.float32)
    for g, w in enumerate((2, 4, 8, 16)):
        for t in range(16):
            corr[:, 16 * g + t] = float(w) / float(min(t + 1, w))
    shared = dict(vecs=vecs, gf_bc=gf_bc, ident=ident, corr=corr,
                  w_in=f(w_in[0]), w_ra=f(w_ra[0]), w_ix=f(w_ix[0]), w_pool=f(w_pool[0]),
                  w_br_lru=f(w_br_lru[0]), w_br_pool=f(w_br_pool[0]), w_out=f(w_out[0]),
                  w_up=f(w_up[0]), w_down=f(w_down[0]))
    in_maps = []
    for i in range(NCORES):
        sl = slice(2 * i, 2 * i + 2)
        m = dict(shared)
        m["xp"] = x_prompt[i]
        m["xs"] = x_sample[sl].reshape(2 * DEC, D)
        m["st_h"] = np.ascontiguousarray(_fm(state_lru_h[0, sl]))
        m["st_r"] = np.ascontiguousarray(np.transpose(_fm(state_lru_conv[0, sl]), (0, 1, 3, 2)).reshape(128, 2, 24))
        m["st_p"] = np.ascontiguousarray(np.transpose(_fm(state_pool[0, sl]), (0, 1, 3, 2)).reshape(128, 2, 120))
        m["st_f"] = np.ascontiguousarray(np.transpose(_fm(state_ffn_conv[0, sl]), (0, 1, 3, 2)).reshape(128, 2, 96))
        in_maps.append(m)

    if "nc" not in _NC_CACHE:
        _NC_CACHE["nc"] = build_all()
    nc = _NC_CACHE["nc"]
    res = run_bass_kernel_spmd(nc, in_maps, core_ids=list(range(NCORES)))
    R = res.results

    def unfm(a, nch, k):
        a = np.asarray(a).reshape(128, nch, k)
        return np.ascontiguousarray(np.transpose(a, (2, 1, 0)).reshape(k, nch * 128))

    y_prompt = np.stack([np.asarray(R[i]["yp"]) for i in range(NCORES)], 0).astype(np.float32)
    y_sample = np.concatenate([np.asarray(R[i]["ys"]).reshape(2, DEC, D) for i in range(NCORES)], 0).astype(np.float32)
    p_h = np.stack([unfm(R[i]["o_h"][:, 0, :], 8, 1)[0] for i in range(NCORES)], 0)[None]
    p_lru = np.stack([unfm(R[i]["o_r"][:, 0, :], 8, 3) for i in range(NCORES)], 0)[None]
    p_pool = np.stack([unfm(R[i]["o_p"][:, 0, :], 8, 15) for i in range(NCORES)], 0)[None]
    p_ffn = np.stack([unfm(R[i]["o_f"][:, 0, :], 48, 2) for i in range(NCORES)], 0)[None]
    s_h = np.stack([unfm(R[i]["o_h"][:, 1 + s, :], 8, 1)[0] for i in range(NCORES) for s in range(2)], 0)[None]
    s_lru = np.stack([unfm(R[i]["o_r"][:, 1 + s, :], 8, 3) for i in range(NCORES) for s in range(2)], 0)[None]
    s_pool = np.stack([unfm(R[i]["o_p"][:, 1 + s, :], 8, 15) for i in range(NCORES) for s in range(2)], 0)[None]
    s_ffn = np.stack([unfm(R[i]["o_f"][:, 1 + s, :], 48, 2) for i in range(NCORES) for s in range(2)], 0)[None]
    outs = (y_prompt, y_sample, p_h, p_lru, p_pool, p_ffn, s_h, s_lru, s_pool, s_ffn)
    return tuple(np.ascontiguousarray(o, dtype=np.float32) for o in outs)
```

```python
import contextlib
import numpy as np
import concourse.bass as bass
import concourse.mybir as mybir
from concourse.bass_utils import run_bass_kernel_spmd

F32 = mybir.dt.float32
BF16 = mybir.dt.bfloat16
AF = mybir.ActivationFunctionType
ALU = mybir.AluOpType

ENGS = ("pe", "act", "dve", "pool", "sp")

D = 1024
KC = 8
DIN = 4096
DFF = 3072
NFFC = 48
SEQ = 4096
TT = 512
NPT = SEQ // TT
DEC = 16
EPS = 1e-6
NCORES = 8

V_G1, V_G2, V_CLW, V_CLB, V_BA, V_BX, V_LAM, V_PSC, V_CFW, V_CFB, NV = 0, 8, 16, 48, 56, 64, 72, 80, 88, 232, 280
DV_HBA, DV_HBX, DV_CH, DV_MH, DV_TMP, NDV = 0, 8, 16, 24, 25, 40

SAME_ENGINE_SYNC = True
HALO_IN_ENG = "act"
CAST_ENG = "dve"
PRIO = "prog"
TBL_AWARE = False
TAP_SRC = "sbuf"
DROP_S2 = False
FFN_TAP_V = "act"
FFN_TAP_G = "act"
COARSE_BUFS = ()
REORDER_OFF = ()
NS_E = 9
NS_U = 6
NS_L = 8
NSB = 4
NW = 7
NPS = 4


class Buf:
    __slots__ = ("name", "w", "r", "gen")

    def __init__(self, name):
        self.name = name
        self.w = None
        self.r = []
        self.gen = 0


class TB:
    __slots__ = ("t", "b", "gen")

    def __init__(self, t, b, gen=0):
        self.t = t
        self.b = b
        self.gen = gen


class CT:
    def __init__(self, t, name, n):
        self.t = t
        if name in COARSE_BUFS:
            b = Buf(name)
            self.c = [TB(t, b) for i in range(n)]
        else:
            self.c = [TB(t, Buf(f"{name}.{i}")) for i in range(n)]


class Op:
    __slots__ = ("idx", "eng", "fn", "preds", "dur", "dsem", "lat", "tbl")


DEF_DUR = {"pe": 1.8, "act": 0.65, "dve": 0.75, "pool": 0.45, "sp": 0.08}
SYNC_LAT = 0.25


class FW:
    def __init__(self, nc):
        self.nc = nc
        self.ops = []
        self.dsem_group = {}
        self.dsem_cnt = {}
        self.dord = {}
        self.sems = {}
        self.scale = 1.0

    def dma_sem(self, key, group=False):
        assert key not in self.dsem_group
        self.dsem_group[key] = group
        self.dsem_cnt[key] = 0
        return key

    def _add(self, e, fn, reads, writes, dur, dsem, lat):
        preds = set()
        for tb in reads:
            assert tb.gen == tb.b.gen, f"stale ring buffer {tb.b.name}"
            if tb.b.w is not None:
                preds.add(tb.b.w)
        for tb in writes:
            assert tb.gen == tb.b.gen, f"stale ring buffer {tb.b.name}"
            b = tb.b
            if b.w is not None:
                preds.add(b.w)
            preds.update(b.r)
        o = Op()
        o.idx = len(self.ops)
        o.eng = e
        o.fn = fn
        o.preds = preds
        o.dur = dur
        o.dsem = dsem
        o.lat = lat
        o.tbl = None
        self.ops.append(o)
        ws = set()
        for tb in writes:
            tb.b.w = o.idx
            tb.b.r = []
            ws.add(id(tb.b))
        for tb in reads:
            if id(tb.b) not in ws:
                tb.b.r.append(o.idx)
        return o

    def op(self, e, fn, reads=(), writes=(), d=None, tbl=None):
        if d is None:
            d = max(0.12, DEF_DUR[e] * self.scale)
        self._add(e, fn, reads, writes, d, None, 0.0).tbl = tbl

    def dma(self, q, fn, dsem, reads=(), writes=(), lat=2.0, nbytes=0):
        dur = max(DEF_DUR["sp"], nbytes / 360e3) if q == "sp" else 1.0
        o = self._add(q, fn, reads, writes, dur, dsem, lat)
        self.dsem_cnt[dsem] += 1
        self.dord[o.idx] = self.dsem_cnt[dsem]

    def final_wait(self, e, tbs):
        self._add(e, None, tbs, tbs, 0.01, None, 0.0)

    def schedule(self):
        import heapq
        ops = self.ops
        N = len(ops)
        members = {}
        for o in ops:
            if o.dsem is not None and self.dsem_group[o.dsem]:
                members.setdefault(o.dsem, []).append(o.idx)
        for o in ops:
            extra = set()
            for p in o.preds:
                ds = ops[p].dsem
                if ds is not None and self.dsem_group[ds] and o.dsem != ds:
                    extra.update(members[ds])
            o.preds |= extra
            if o.dsem is not None and self.dsem_group[o.dsem]:
                o.preds = set(p for p in o.preds if ops[p].dsem != o.dsem)
        succ = [[] for _ in range(N)]
        npred = [0] * N
        for o in ops:
            npred[o.idx] = len(o.preds)
            for p in o.preds:
                succ[p].append(o.idx)
        ready_t = [0.0] * N
        fin = [0.0] * N
        if PRIO == "blevel":
            bl = [0.0] * N
            for o in reversed(ops):
                m = 0.0
                for sidx in succ[o.idx]:
                    if bl[sidx] > m:
                        m = bl[sidx]
                bl[o.idx] = m + o.dur + o.lat
            key = [(-bl[i], i) for i in range(N)]
        else:
            key = [(i, i) for i in range(N)]
        free = {e: 0.0 for e in ENGS}
        avail = {e: [] for e in ENGS}
        now = {e: [] for e in ENGS}
        order = {e: [] for e in ENGS}
        cur_tbl = [None]
        for o in ops:
            if npred[o.idx] == 0:
                heapq.heappush(avail[o.eng], (0.0, key[o.idx], o.idx))
        done = 0

        def pick_now(e):
            nw = now[e]
            if e != "act" or not TBL_AWARE or len(nw) < 2:
                return nw[0]
            best = None
            for cand in heapq.nsmallest(5, nw):
                t = ops[cand[1]].tbl
                if t is None or t == cur_tbl[0]:
                    best = cand
                    break
            return best if best is not None else nw[0]

        while done < N:
            best = None
            for e in ENGS:
                av, nw, fr = avail[e], now[e], free[e]
                while av and av[0][0] <= fr:
                    it = heapq.heappop(av)
                    heapq.heappush(nw, (it[1], it[2]))
                if nw:
                    pk = pick_now(e)
                    cand = (fr, pk[0], e, 1, pk)
                elif av:
                    cand = (av[0][0], av[0][1], e, 0, av[0])
                else:
                    continue
                if best is None or (cand[0], cand[1]) < (best[0], best[1]):
                    best = cand
            assert best is not None, "scheduler stuck (cyclic deps?)"
            start, _, e, fromnow, item = best
            if fromnow:
                now[e].remove(item)
                heapq.heapify(now[e])
                i = item[1]
            else:
                heapq.heappop(avail[e])
                i = item[2]
            o = ops[i]
            if e == "act" and o.tbl is not None:
                if cur_tbl[0] is not None and cur_tbl[0] != o.tbl:
                    start += 1.28
                cur_tbl[0] = o.tbl
            free[e] = start + o.dur
            fin[i] = start + o.dur + o.lat
            order[e].append(i)
            done += 1
            for sidx in succ[i]:
                t = fin[i] + (SYNC_LAT if ops[sidx].eng != e or o.dsem is not None else 0.05)
                if t > ready_t[sidx]:
                    ready_t[sidx] = t
                npred[sidx] -= 1
                if npred[sidx] == 0:
                    heapq.heappush(avail[ops[sidx].eng], (ready_t[sidx], key[sidx], sidx))
        self.makespan = max(fin)
        for e in REORDER_OFF:
            order[e] = sorted(order[e])
        self.order = order
        return order

    def build(self, st):
        nc = self.nc
        ops = self.ops
        order = self.schedule()
        ev = {}
        for e in ENGS:
            n = 0
            for i in order[e]:
                o = ops[i]
                if o.dsem is None:
                    if o.fn is not None:
                        n += 1
                        ev[i] = (e, n)
                    else:
                        ev[i] = None
                else:
                    k = self.dsem_cnt[o.dsem] if self.dsem_group[o.dsem] else self.dord[i]
                    ev[i] = (o.dsem, 16 * k)
        streams = {e: [] for e in ENGS}
        for e in ENGS:
            waited = {}
            for i in order[e]:
                o = ops[i]
                need = {}
                for p in o.preds:
                    pe = ev[p]
                    if pe is None:
                        continue
                    k, v = pe
                    if k == e and (not SAME_ENGINE_SYNC or e == "pe"):
                        continue
                    if o.dsem is not None and k == o.dsem and self.dsem_group[k]:
                        continue
                    if need.get(k, 0) < v:
                        need[k] = v
                wl = []
                for k, v in need.items():
                    if waited.get(k, 0) < v:
                        waited[k] = v
                        wl.append((k, v))
                streams[e].append((i, wl))
        val = {}
        ptr = {e: 0 for e in ENGS}
        progress = True
        while progress:
            progress = False
            for e in ENGS:
                while ptr[e] < len(streams[e]):
                    i, wl = streams[e][ptr[e]]
                    if all(val.get(k, 0) >= v for k, v in wl):
                        o = ops[i]
                        if o.dsem is not None:
                            val[o.dsem] = val.get(o.dsem, 0) + 16
                        elif o.fn is not None:
                            val[e] = val.get(e, 0) + 1
                        ptr[e] += 1
                        progress = True
                    else:
                        break
        if not all(ptr[e] == len(streams[e]) for e in ENGS):
            msg = []
            for e in ENGS:
                if ptr[e] < len(streams[e]):
                    i, wl = streams[e][ptr[e]]
                    msg.append(f"{e}: op#{i} pos {ptr[e]}/{len(streams[e])} waits {[(k, v, val.get(k, 0)) for k, v in wl if val.get(k, 0) < v]} preds {sorted(ops[i].preds)[-6:]}")
            raise AssertionError("semaphore deadlock in generated program: " + " | ".join(msg))
        for k in list(ENGS) + list(self.dsem_group.keys()):
            self.sems[k] = st.enter_context(nc.semaphore("s_" + str(k)))
        block = st.enter_context(nc.Block())
        sems = self.sems

        def run(eng, e):
            for i, wl in streams[e]:
                for k, v in wl:
                    eng.wait_ge(sems[k], v)
                o = ops[i]
                if o.fn is None:
                    continue
                if o.dsem is not None:
                    o.fn(eng).then_inc(sems[o.dsem], 16)
                else:
                    o.fn(eng).then_inc(sems[e], 1)

        @block.sync
        def _(eng):
            run(eng, "sp")

        @block.tensor
        def _(eng):
            run(eng, "pe")

        @block.scalar
        def _(eng):
            run(eng, "act")

        @block.vector
        def _(eng):
            run(eng, "dve")

        @block.gpsimd
        def _(eng):
            run(eng, "pool")


class Ring:
    def __init__(self, tbs):
        self.tbs = tbs
        self.i = 0

    def grab(self):
        tb = self.tbs[self.i % len(self.tbs)]
        self.i += 1
        tb.b.gen += 1
        return TB(tb.t, tb.b, tb.b.gen)


def build_program(worder=None):
    nc = bass.Bass("TRN2", target_bir_lowering=False)
    fw = FW(nc)
    st = contextlib.ExitStack()

    def din(name, shape, dt=F32):
        return nc.dram_tensor(name, list(shape), dt, kind="ExternalInput").ap()

    def dout(name, shape, dt=F32):
        return nc.dram_tensor(name, list(shape), dt, kind="ExternalOutput").ap()

    def dint(name, shape, dt):
        return nc.dram_tensor(name, list(shape), dt).ap()

    xp_d = din("xp", [SEQ, D])
    xs_d = din("xs", [2 * DEC, D])
    sth_d = din("st_h", [128, 2, 8])
    str_d = din("st_r", [128, 2, 24])
    stp_d = din("st_p", [128, 2, 120])
    stf_d = din("st_f", [128, 2, 96])
    vecs_d = din("vecs", [128, NV])
    gf_d = din("gf_bc", [128, D])
    id_d = din("ident", [128, 128])
    corr_d = din("corr", [128, 64])
    w_in_d = din("w_in", [D, DIN])
    w_ra_d = din("w_ra", [16, 64, 64])
    w_ix_d = din("w_ix", [16, 64, 64])
    w_pool_d = din("w_pool", [4, 256, 256])
    w_brl_d = din("w_br_lru", [D, D])
    w_brp_d = din("w_br_pool", [D, D])
    w_out_d = din("w_out", [D, D])
    w_up_d = din("w_up", [D, 2 * DFF])
    w_down_d = din("w_down", [DFF, D])

    wb_in = dint("wb_in", [D, DIN], BF16)
    wb_brl = dint("wb_brl", [D, D], BF16)
    wb_brp = dint("wb_brp", [D, D], BF16)
    wb_out = dint("wb_out", [D, D], BF16)
    wb_up = dint("wb_up", [D, 2 * DFF], BF16)
    wb_down = dint("wb_down", [DFF, D], BF16)

    yp_d = dout("yp", [SEQ, D])
    ys_d = dout("ys", [2 * DEC, D])
    oh_d = dout("o_h", [128, 3, 8])
    or_d = dout("o_r", [128, 3, 24])
    op_d = dout("o_p", [128, 3, 120])
    of_d = dout("o_f", [128, 3, 96])

    def sb(name, shape, dt=F32):
        t = st.enter_context(nc.sbuf_tensor("sb_" + name, list(shape), dt))
        return TB(t, Buf(name))

    def ps(name, shape):
        t = st.enter_context(nc.psum_tensor("pp_" + name, list(shape), F32))
        return TB(t, Buf(name))

    vecs = sb("vecs", [128, NV])
    dv = sb("dv", [128, NDV])
    gf = sb("gf", [128, D])
    ident = sb("ident", [128, 128])
    corr = sb("corr", [128, 64])
    wra = sb("wra", [128, 8, 128], BF16)
    wix = sb("wix", [128, 8, 128], BF16)
    wpl = sb("wpl", [128, 4, 2, 256], BF16)
    states = []
    for q in range(2):
        ns_ = 1 + q
        a0 = sb(f"hst{q}", [128, ns_, 8])
        a1 = sb(f"hal_r{q}", [128, ns_, 8, 3])
        a2 = sb(f"hal_p{q}", [128, ns_, 8, 15])
        a3 = sb(f"hal_f{q}", [128, ns_, NFFC, 2])
        states.append((CT(a0.t, f"hst{q}", 8), CT(a1.t, f"hal_r{q}", 8), CT(a2.t, f"hal_p{q}", 8), CT(a3.t, f"hal_f{q}", NFFC)))
    xres = [[sb(f"xres{p}_{b}", [128, D]) for b in range(4)] for p in range(2)]
    dgr = Ring([sb(f"dg{i}", [128, 128]) for i in range(3)])
    NXT = 2
    xtmp = [sb(f"xtmp{i}", [128, D]) for i in range(NXT)]
    junk = sb("junk", [128, D], BF16)
    stat = Ring([sb(f"stat{i}", [128, 2]) for i in range(10)])
    statL = Ring([sb(f"statL{i}", [128, 2]) for i in range(6)])
    xnT = sb("xnT", [128, KC, TT], BF16); xnT = CT(xnT.t, "xnT", KC)
    x1nT = sb("x1nT", [128, KC, TT], BF16); x1nT = CT(x1nT.t, "x1nT", KC)
    hT = sb("hT", [128, KC, TT], BF16); hT = CT(hT.t, "hT", KC)
    ppT = sb("ppT", [128, KC, TT], BF16); ppT = CT(ppT.t, "ppT", KC)
    mgT = sb("mgT", [128, KC, TT], BF16); mgT = CT(mgT.t, "mgT", KC)
    actT = sb("actT", [128, DFF // 128, TT], BF16); actT = CT(actT.t, "actT", DFF // 128)
    S = Ring([sb(f"S{i}", [128, 528]) for i in range(NS_E)])
    SU = Ring([sb(f"SU{i}", [128, TT]) for i in range(NS_U)])
    SL = Ring([sb(f"SL{i}", [128, 516]) for i in range(NS_L)])
    SB_ = Ring([sb(f"SB{i}", [128, TT], BF16) for i in range(NSB)])
    plb = Ring([sb(f"plb{i}", [128, 2, TT], BF16) for i in range(2)])
    wslots = [sb(f"wsl{i}", [128, 2048], BF16) for i in range(NW)]
    wsem = [fw.dma_sem(f"w{i}") for i in range(NW)]
    PS = Ring([ps(f"ps{i}", [128, 512]) for i in range(NPS)])
    PSL = Ring([ps(f"psl{i}", [128, 512]) for i in range(NPS)])

    s_const = fw.dma_sem("const", group=True)
    s_cast = {k: fw.dma_sem("cast_" + k, group=(k == "small")) for k in ("small",)}
    s_x = [[fw.dma_sem(f"x{p}_{b}") for b in range(4)] for p in range(2)]
    s_xt = [fw.dma_sem(f"xt{i}") for i in range(NXT)]
    s_sm = [fw.dma_sem(f"small{i}") for i in range(3)]
    s_st = [fw.dma_sem(f"stin{q}", group=True) for q in range(3)]
    s_so = [fw.dma_sem(f"stout{q}", group=True) for q in range(3)]
    s_stf = [fw.dma_sem(f"stinf{q}") for q in range(3)]
    s_sof = [fw.dma_sem(f"stoutf{q}") for q in range(3)]

    scr = {}

    for tb, d in ((vecs, vecs_d), (gf, gf_d), (ident, id_d), (corr, corr_d)):
        fw.dma("sp", lambda e, tb=tb, d=d: e.dma_start(out=tb.t[:], in_=d), s_const, writes=[tb])

    cast_t = [0.0]

    def cast_piece(key, src, dst, c0, ncol):
        rows = src.shape[0]
        s2 = src[:, c0:c0 + ncol]
        d2 = dst[:, c0:c0 + ncol]
        if ncol > 1024:
            a = ncol // 1024
            s2 = s2.rearrange("k (a n) -> k a n", a=a)
            d2 = d2.rearrange("k (a n) -> k a n", a=a)
        cast_t[0] += rows * ncol * 6 / 330e3
        fw.dma("pool", lambda e: e.dma_start(out=d2, in_=s2), s_cast[key], writes=[scr[key]], lat=4.0 + cast_t[0])

    fw.op("dve", lambda e: e.memset(wra.t[:], 0.0), writes=[wra])
    fw.op("dve", lambda e: e.memset(wix.t[:], 0.0), writes=[wix])
    for (wt, wd, sk) in ((wra, w_ra_d, s_sm[0]), (wix, w_ix_d, s_sm[1])):
        for j in range(2):
            src = wd.rearrange("(c j) k d -> j k c d", j=2)[j]
            fw.dma("pool", lambda e, wt=wt, src=src, j=j: e.dma_start(out=wt.t[64 * j:64 * j + 64, :, 64 * j:64 * j + 64], in_=src),
                   sk, writes=[wt])
    fw.dma("pool", lambda e: e.dma_start(out=wpl.t[:], in_=w_pool_d.rearrange("g (kc p) n -> p g kc n", p=128)),
           s_sm[2], writes=[wpl])

    V = vecs.t
    DVt = dv.t
    fw.op("dve", lambda e: e.tensor_scalar(out=DVt[:, DV_HBA:DV_HBA + 8], in0=V[:, V_BA:V_BA + 8], scalar1=0.5, scalar2=None, op0=ALU.mult), reads=[vecs], writes=[dv])
    fw.op("dve", lambda e: e.tensor_scalar(out=DVt[:, DV_HBX:DV_HBX + 8], in0=V[:, V_BX:V_BX + 8], scalar1=0.5, scalar2=None, op0=ALU.mult), reads=[vecs], writes=[dv])
    fw.op("dve", lambda e: e.memset(DVt[:, DV_MH:DV_MH + 1], -0.5), writes=[dv])
    fw.op("act", lambda e: e.activation(out=DVt[:, DV_TMP:DV_TMP + 8], in_=V[:, V_LAM:V_LAM + 8], func=AF.Exp, scale=-1.0), reads=[vecs], writes=[dv])
    fw.op("act", lambda e: e.activation(out=DVt[:, DV_TMP:DV_TMP + 8], in_=DVt[:, DV_TMP:DV_TMP + 8], func=AF.Ln, bias=1.0, scale=1.0), reads=[dv], writes=[dv])
    fw.op("dve", lambda e: e.tensor_scalar(out=DVt[:, DV_CH:DV_CH + 8], in0=DVt[:, DV_TMP:DV_TMP + 8], scalar1=-4.0, scalar2=None, op0=ALU.mult), reads=[dv], writes=[dv])

    def vcol(c):
        return V[:, c:c + 1]

    def dcol(c):
        return DVt[:, c:c + 1]

    def kblock(mat, c0, nb):
        return mat.rearrange("(kc p) n -> p kc n", p=128)[:, :, c0:c0 + nb]

    def wdesc(key):
        kind = key[0]
        if kind in ("in", "brl", "brp", "up"):
            mb, mf = {"in": (wb_in, w_in_d), "brl": (wb_brl, w_brl_d), "brp": (wb_brp, w_brp_d), "up": (wb_up, w_up_d)}[kind]
            return (kblock(mb, 256 * key[1], 256), kblock(mf, 256 * key[1], 256), 8, 256)
        mb, mf = (wb_out, w_out_d) if kind == "out" else (wb_down, w_down_d)
        h, kg = key[1], key[2]
        sl = (slice(None), slice(4 * kg, 4 * kg + 4), slice(512 * h, 512 * h + 512))
        return (mb.rearrange("(kc p) n -> p kc n", p=128)[sl], mf.rearrange("(kc p) n -> p kc n", p=128)[sl], 4, 512)

    class WStream:
        def __init__(self, order):
            self.record = order is None
            self.order = [] if order is None else order
            self.issued = 0
            self.consumed = 0
            self.free = list(range(NW))
            self.loaded = {}

        def pump(self):
            if self.record:
                return
            while self.free and self.issued < len(self.order):
                key = self.order[self.issued]
                ap_b, ap_f, nk, nb = wdesc(key)
                slot = self.free.pop(0)
                tbs = wslots[slot]
                tbs.b.gen += 1
                tb = TB(tbs.t, tbs.b, tbs.b.gen)
                dst = tb.t[:, 0:nk * nb].rearrange("p (k n) -> p k n", n=nb)
                nby = 128 * nk * nb * 2
                if key not in scrb:
                    scrb[key] = TB(None, Buf("scr_" + str(key)))
                    fw.dma("pool", lambda e, dst=dst, ap_f=ap_f: e.dma_start(out=dst, in_=ap_f), s_wsw[slot], writes=[tb], lat=2.0 + 3 * nby / 360e3)
                    fw.dma("sp", lambda e, dst=dst, ap_b=ap_b: e.dma_start(out=ap_b, in_=dst), s_wb[slot], reads=[tb], writes=[scrb[key]], nbytes=nby)
                else:
                    fw.dma("sp", lambda e, dst=dst, ap_b=ap_b: e.dma_start(out=dst, in_=ap_b), wsem[slot], reads=[scrb[key]], writes=[tb], nbytes=nby)
                self.loaded[self.issued] = (slot, tb)
                self.issued += 1

        def next(self, key):
            if self.record:
                self.order.append(key)
                tbs = wslots[0]
                return (0, TB(tbs.t, tbs.b, tbs.b.gen))
            self.pump()
            assert self.order[self.consumed] == key, (self.order[self.consumed], key)
            slot, tb = self.loaded.pop(self.consumed)
            self.consumed += 1
            return (slot, tb)

        def done(self, h):
            if self.record:
                return
            self.free.append(h[0])
            self.pump()

    scrb = {}
    s_wb = [fw.dma_sem(f"wb{i}") for i in range(NW)]
    s_wsw = [fw.dma_sem(f"wsw{i}") for i in range(NW)]
    ws = WStream(worder)

    ptiles = [dict(seq=0, T=TT, first=(k == 0), last=(k == NPT - 1), xd=xp_d, yd=yp_d, r0=k * TT, prompt=True) for k in range(NPT)]
    for t_ in ptiles:
        t_["nseg"], t_["L"] = 1, TT
    stiles = [dict(seq=1, T=2 * DEC, first=True, last=True, xd=xs_d, yd=ys_d, r0=0, prompt=False, nseg=2, L=DEC)]
    tiles = ptiles[0:8] + [stiles[0]] + ptiles[8:]

    def seg3(ap2d, width, nseg):
        return ap2d.rearrange("p (s w) -> p s w", w=width)

    def blocks_of(T):
        return [(b * 128, min(128, T - b * 128)) for b in range((T + 127) // 128)]

    def norm_p1(xb, bs, ring):
        s = ring.grab()
        fw.op("act", lambda e: e.activation(out=junk.t[0:bs, :], in_=xb.t[0:bs, :], func=AF.Square, accum_out=s.t[0:bs, 0:1]),
              reads=[xb], writes=[junk, s], d=1.1)
        fw.op("dve", lambda e: e.tensor_scalar(out=s.t[0:bs, 0:1], in0=s.t[0:bs, 0:1], scalar1=1.0 / D, scalar2=EPS, op0=ALU.mult, op1=ALU.add),
              reads=[s], writes=[s], d=0.15)
        fw.op("pool", lambda e: e.tensor_tensor(out=s.t[0:bs, 1:2], in0=s.t[0:bs, 0:1], in1=DVt[0:bs, DV_MH:DV_MH + 1], op=ALU.pow),
              reads=[s, dv], writes=[s], d=0.5)
        return s

    def norm_p2(xb, bs, s, gcol, dstT, col0):
        dg = dgr.grab()
        fw.op("dve", lambda e: e.tensor_scalar(out=dg.t[0:bs, 0:bs], in0=ident.t[0:bs, 0:bs], scalar1=s.t[0:bs, 1:2], scalar2=None, op0=ALU.mult),
              reads=[s, ident], writes=[dg], d=0.15)
        pa = PS.grab()
        pb = PS.grab()

        def tr(e):
            for c in range(KC):
                pt = pa if c < 4 else pb
                i = e.matmul(pt.t[:, (c % 4) * 128:(c % 4) * 128 + bs], lhsT=xb.t[0:bs, c * 128:(c + 1) * 128], rhs=dg.t[0:bs, 0:bs], start=True, stop=True)
            return i
        fw.op("pe", tr, reads=[xb, dg], writes=[pa, pb], d=1.9 * (bs / 128.0) + 0.2)

        def ev_d(e):
            for c in range(0, 4):
                i = e.tensor_scalar(out=dstT.t[:, c, col0:col0 + bs], in0=pa.t[:, c * 128:c * 128 + bs], scalar1=vcol(gcol + c), scalar2=None, op0=ALU.mult)
            return i

        def ev_a(e):
            for c in range(4, KC):
                i = e.activation(out=dstT.t[:, c, col0:col0 + bs], in_=pb.t[:, (c - 4) * 128:(c - 4) * 128 + bs], func=AF.Identity, scale=vcol(gcol + c))
            return i
        fw.op("dve", ev_d, reads=[pa, vecs], writes=[dstT.c[c] for c in range(0, 4)], d=1.1 * fw.scale + 0.1)
        fw.op("act", ev_a, reads=[pb, vecs], writes=[dstT.c[c] for c in range(4, KC)], d=1.1 * fw.scale + 0.1)

    def load_tile(ti, par):
        T = ti["T"]
        for b, (c0, bs) in enumerate(blocks_of(T)):
            xb = xres[par][b]
            src = ti["xd"][ti["r0"] + c0: ti["r0"] + c0 + bs, :]
            fw.dma("sp", lambda e, xb=xb, src=src, bs=bs: e.dma_start(out=xb.t[0:bs, :], in_=src), s_x[par][b], writes=[xb], nbytes=bs * D * 4)

    def proj_fm(wtb, cs, srcT, T, ring):
        p = ring.grab()
        wv = wtb.t[:, 0:2048].rearrange("p (k n) -> p k n", n=256)

        def mm(e):
            for k in range(KC):
                i = e.matmul(p.t[:, 0:T], lhsT=wv[:, k, cs:cs + 128], rhs=srcT.t[:, k, 0:T], start=(k == 0), stop=(k == KC - 1))
            return i
        fw.op("pe", mm, reads=[wtb] + srcT.c, writes=[p], d=0.22 * 8 * fw.scale + 0.1)
        return p

    def tokproj(ti, par, srcT, nk, wkey, epi, ring):
        T = ti["T"]
        blks = blocks_of(T)
        cost = 0.22 * 4 * len(blks) if T > 32 else 1.5
        for h in range(2):
            pss = [ring.grab() for _ in blks]
            for kg in range(nk // 4):
                wh = ws.next((wkey, h, kg))
                wv = wh[1].t[:, 0:2048].rearrange("p (k n) -> p k n", n=512)

                def mm(e, kg=kg, wv=wv, pss=pss):
                    for kk in range(4):
                        kc = kg * 4 + kk
                        for b, (c0, bs) in enumerate(blks):
                            i = e.matmul(pss[b].t[0:bs, 0:512], lhsT=srcT.t[:, kc, c0:c0 + bs], rhs=wv[:, kk, :], start=(kc == 0), stop=(kc == nk - 1))
                    return i
                fw.op("pe", mm, reads=[wh[1]] + srcT.c[4 * kg:4 * kg + 4], writes=pss, d=0.22 * 4 * len(blks) * fw.scale + 0.1)
                ws.done(wh)
                yield cost
            for b, (c0, bs) in enumerate(blks):
                epi(b, bs, h, pss[b])
            yield 0.1

    def init_state_early(ti):
        hst, hal_r, hal_p, hal_f = states[0 if ti["prompt"] else 1]
        if ti["prompt"]:
            fw.op("pool", lambda e: e.memset(hst.t[:], 0.0), writes=hst.c, d=0.1)
            fw.op("pool", lambda e: e.memset(hal_r.t[:], 0.0), writes=hal_r.c, d=0.1)
            fw.op("pool", lambda e: e.memset(hal_p.t[:], 0.0), writes=hal_p.c, d=0.15)
        else:
            fw.dma("sp", lambda e: e.dma_start(out=hst.t[:], in_=sth_d[:, :, :]), s_st[ti["seq"]], writes=hst.c)
            fw.dma("sp", lambda e: e.dma_start(out=hal_r.t[:].rearrange("p s c k -> p s (c k)"), in_=str_d[:, :, :]), s_st[ti["seq"]], writes=hal_r.c)
            fw.dma("sp", lambda e: e.dma_start(out=hal_p.t[:].rearrange("p s c k -> p s (c k)"), in_=stp_d[:, :, :]), s_st[ti["seq"]], writes=hal_p.c)

    def init_state_late(ti):
        hst, hal_r, hal_p, hal_f = states[0 if ti["prompt"] else 1]
        if ti["prompt"]:
            fw.op("pool", lambda e: e.memset(hal_f.t[:], 0.0), writes=hal_f.c, d=0.15)
        else:
            fw.dma("sp", lambda e: e.dma_start(out=hal_f.t[:].rearrange("p s c k -> p s (c k)"), in_=stf_d[:, :, :]), s_stf[ti["seq"]], writes=hal_f.c)

    def store_state_early(ti):
        hst, hal_r, hal_p, hal_f = states[0 if ti["prompt"] else 1]
        q = ti["seq"]
        q1 = q + ti["nseg"]
        fw.dma("sp", lambda e: e.dma_start(out=oh_d[:, q:q1, :], in_=hst.t[:]), s_so[q], reads=hst.c)
        fw.dma("sp", lambda e: e.dma_start(out=or_d[:, q:q1, :], in_=hal_r.t[:].rearrange("p s c k -> p s (c k)")), s_so[q], reads=hal_r.c)
        fw.dma("sp", lambda e: e.dma_start(out=op_d[:, q:q1, :], in_=hal_p.t[:].rearrange("p s c k -> p s (c k)")), s_so[q], reads=hal_p.c)

    def store_state_late(ti):
        hst, hal_r, hal_p, hal_f = states[0 if ti["prompt"] else 1]
        q = ti["seq"]
        q1 = q + ti["nseg"]
        fw.dma("sp", lambda e: e.dma_start(out=of_d[:, q:q1, :], in_=hal_f.t[:].rearrange("p s c k -> p s (c k)")), s_sof[q], reads=hal_f.c)

    def phase_B(ti):
        hst, hal_r, hal_p, hal_f = states[0 if ti["prompt"] else 1]
        T = ti["T"]
        NSG, L = ti["nseg"], ti["L"]
        cmm = 0.22 * 8 if T > 32 else 0.75
        wcur = {}
        st1 = {}
        st2 = {}

        def s1(c):
            if c % 2 == 0:
                wcur[c // 2] = ws.next(("in", c // 2))
            wh = wcur[c // 2]
            p = proj_fm(wh[1], (c % 2) * 128, xnT, T, PS)
            if c % 2 == 1:
                ws.done(wh)
            xr = S.grab()
            xr3 = seg3(xr.t[:, 0:NSG * (3 + L)], 3 + L, NSG)
            fw.op(HALO_IN_ENG, lambda e: (e.copy if HALO_IN_ENG == "act" else e.tensor_copy)(out=xr3[:, :, 0:3], in_=hal_r.t[:, :, c, :]), reads=[hal_r.c[c]], writes=[xr], d=0.2)
            fw.op("act", lambda e: e.activation(out=xr3[:, :, 3:3 + L], in_=seg3(p.t[:, 0:T], L, NSG), func=AF.Copy), reads=[p], writes=[xr])
            u = SU.grab()
            if TAP_SRC == "psum":
                fw.op("act", lambda e: e.activation(out=u.t[:, 0:T], in_=p.t[:, 0:T], func=AF.Identity, scale=vcol(V_CLW + 4 * c + 3), bias=vcol(V_CLB + c)),
                      reads=[p, vecs], writes=[u])
            else:
                fw.op("act", lambda e: e.activation(out=seg3(u.t[:, 0:T], L, NSG), in_=xr3[:, :, 3:3 + L], func=AF.Identity, scale=vcol(V_CLW + 4 * c + 3), bias=vcol(V_CLB + c)),
                      reads=[xr, vecs], writes=[u], d=0.55 * fw.scale + 0.1)
            u3 = seg3(u.t[:, 0:T], L, NSG)
            for j in range(3):
                fw.op("dve", lambda e, j=j: e.scalar_tensor_tensor(out=u3, in0=xr3[:, :, j:j + L], scalar=vcol(V_CLW + 4 * c + j), in1=u3, op0=ALU.mult, op1=ALU.add),
                      reads=[xr, u, vecs], writes=[u])
            fw.op("act", lambda e: e.copy(out=hal_r.t[:, :, c, :], in_=xr3[:, :, L:L + 3]), reads=[xr], writes=[hal_r.c[c]], d=0.2)
            ub = SB_.grab()
            if CAST_ENG == "act":
                fw.op("act", lambda e: e.activation(out=ub.t[:, 0:T], in_=u.t[:, 0:T], func=AF.Copy), reads=[u], writes=[ub])
            else:
                fw.op("dve", lambda e: e.tensor_copy(out=ub.t[:, 0:T], in_=u.t[:, 0:T]), reads=[u], writes=[ub], d=0.3 * fw.scale + 0.08)
            st1[c] = (u, ub)

        def s2a(c0):
            loc = []
            for c in (c0, c0 + 1):
                u, ub = st1.pop(c)
                pr = PS.grab()
                fw.op("pe", lambda e, pr=pr, c=c, ub=ub: e.matmul(pr.t[:, 0:T], lhsT=wra.t[:, c, :], rhs=ub.t[:, 0:T], start=True, stop=True), reads=[wra, ub], writes=[pr], d=0.25 * fw.scale + 0.1)
                pi = PS.grab()
                fw.op("pe", lambda e, pi=pi, c=c, ub=ub: e.matmul(pi.t[:, 0:T], lhsT=wix.t[:, c, :], rhs=ub.t[:, 0:T], start=True, stop=True), reads=[wix, ub], writes=[pi], d=0.25 * fw.scale + 0.1)
                A = S.grab()
                fw.op("act", lambda e, A=A, pr=pr, c=c: e.activation(out=A.t[:, 0:T], in_=pr.t[:, 0:T], func=AF.Tanh, scale=0.5, bias=dcol(DV_HBA + c)), reads=[pr, dv], writes=[A], tbl=1)
                fw.op("act", lambda e, A=A, c=c: e.activation(out=A.t[:, 0:T], in_=A.t[:, 0:T], func=AF.Exp, scale=dcol(DV_CH + c), bias=dcol(DV_CH + c)), reads=[A, dv], writes=[A], tbl=1)
                I = S.grab()
                fw.op("act", lambda e, I=I, pi=pi, c=c: e.activation(out=I.t[:, 0:T], in_=pi.t[:, 0:T], func=AF.Tanh, scale=0.5, bias=dcol(DV_HBX + c)), reads=[pi, dv], writes=[I], tbl=1)
                M = S.grab()
                fw.op("act", lambda e, M=M, A=A: e.activation(out=M.t[:, 0:T], in_=A.t[:, 0:T], func=AF.Square), reads=[A], writes=[M])
                loc.append((c, u, A, I, M))
            for (c, u, A, I, M) in loc:
                fw.op("act", lambda e, M=M: e.activation(out=M.t[:, 0:T], in_=M.t[:, 0:T], func=AF.Sqrt, scale=-1.0, bias=1.0), reads=[M], writes=[M], tbl=2)
                st2[c] = (u, A, I, M)

        def s2b(c):
            u, A, I, M = st2.pop(c)
            fw.op("dve", lambda e: e.scalar_tensor_tensor(out=I.t[:, 0:T], in0=I.t[:, 0:T], scalar=1.0, in1=u.t[:, 0:T], op0=ALU.add, op1=ALU.mult), reads=[I, u], writes=[I])
            fw.op("dve", lambda e: e.scalar_tensor_tensor(out=I.t[:, 0:T], in0=I.t[:, 0:T], scalar=0.5, in1=M.t[:, 0:T], op0=ALU.mult, op1=ALU.mult), reads=[I, M], writes=[I])
            h = M
            def scans(e):
                for g_ in range(NSG):
                    i_ = e.tensor_tensor_scan(out=h.t[:, g_ * L:(g_ + 1) * L], data0=A.t[:, g_ * L:(g_ + 1) * L], data1=I.t[:, g_ * L:(g_ + 1) * L],
                                              initial=hst.t[:, g_, c:c + 1], op0=ALU.mult, op1=ALU.add)
                return i_
            fw.op("dve", scans, reads=[A, I, hst.c[c]], writes=[h], d=1.1 * fw.scale + 0.1 * NSG)
            fw.op("act", lambda e: e.copy(out=hst.t[:, :, c:c + 1], in_=seg3(h.t[:, 0:T], L, NSG)[:, :, L - 1:L]), reads=[h], writes=[hst.c[c]], d=0.2)
            if CAST_ENG == "act":
                fw.op("act", lambda e: e.activation(out=hT.t[:, c, 0:T], in_=h.t[:, 0:T], func=AF.Copy), reads=[h], writes=[hT.c[c]])
            else:
                fw.op("dve", lambda e: e.tensor_copy(out=hT.t[:, c, 0:T], in_=h.t[:, 0:T]), reads=[h], writes=[hT.c[c]], d=0.3 * fw.scale + 0.08)

        s1(0)
        yield cmm
        s1(1)
        yield cmm
        for pr_ in range(4):
            c0 = 2 * pr_
            if c0 + 2 < KC:
                s1(c0 + 2)
                yield cmm
                s1(c0 + 3)
                yield cmm
            s2a(c0)
            yield 0.9
            s2b(c0)
            s2b(c0 + 1)
            yield 0.1

    def phase_C(ti):
        hst, hal_r, hal_p, hal_f = states[0 if ti["prompt"] else 1]
        T = ti["T"]
        NSG, LS = ti["nseg"], ti["L"]
        L = 15 + LS
        cmm = 0.22 * 8 if T > 32 else 0.75
        pend = {}

        def s1(g, n, wh, pl):
            c = 2 * g + n
            p = proj_fm(wh[1], n * 128, xnT, T, PS)
            xp = S.grab()
            xp3 = seg3(xp.t[:, 0:NSG * L], L, NSG)
            fw.op(HALO_IN_ENG, lambda e: (e.copy if HALO_IN_ENG == "act" else e.tensor_copy)(out=xp3[:, :, 0:15], in_=hal_p.t[:, :, c, :]), reads=[hal_p.c[c]], writes=[xp], d=0.2)
            fw.op("act", lambda e: e.activation(out=xp3[:, :, 15:L], in_=seg3(p.t[:, 0:T], LS, NSG), func=AF.Copy), reads=[p], writes=[xp])
            fw.op("act", lambda e: e.copy(out=hal_p.t[:, :, c, :], in_=xp3[:, :, LS:LS + 15]), reads=[xp], writes=[hal_p.c[c]], d=0.2)
            cur = xp
            for lv in range(1, g + 2):
                d = 1 << (lv - 1)
                lo = (1 << lv) - 1
                nxt = S.grab()

                def lvl(e, nxt=nxt, cur=cur, lo=lo, d=d):
                    n3 = seg3(nxt.t[:, 0:NSG * L], L, NSG)
                    c3 = seg3(cur.t[:, 0:NSG * L], L, NSG)
                    return e.tensor_tensor(out=n3[:, :, lo:L], in0=c3[:, :, lo:L], in1=c3[:, :, lo - d:L - d], op=ALU.add)
                fw.op("pool", lvl, reads=[cur], writes=[nxt], d=1.05 * fw.scale + 0.1)
                cur = nxt
            w = 1 << (g + 1)
            if ti["prompt"] and ti["first"]:
                fw.op("dve", lambda e: e.tensor_tensor(out=cur.t[:, 15:15 + w - 1], in0=cur.t[:, 15:15 + w - 1], in1=corr.t[:, 16 * g:16 * g + w - 1], op=ALU.mult),
                      reads=[cur, corr], writes=[cur])
            fw.op("dve", lambda e: e.scalar_tensor_tensor(out=seg3(pl.t[:, n, 0:T], LS, NSG), in0=seg3(cur.t[:, 0:NSG * L], L, NSG)[:, :, 15:L], scalar=1.0 / w,
                                                          in1=xp3[:, :, 15:L], op0=ALU.mult, op1=ALU.subtract),
                  reads=[cur, xp], writes=[pl])

        def s2(g):
            pl = pend.pop(g)
            for n in range(2):
                c = 2 * g + n
                p = PS.grab()

                def mm(e, p=p, n=n):
                    for kc in range(2):
                        i = e.matmul(p.t[:, 0:T], lhsT=wpl.t[:, g, kc, n * 128:(n + 1) * 128], rhs=pl.t[:, kc, 0:T], start=(kc == 0), stop=(kc == 1))
                    return i
                fw.op("pe", mm, reads=[wpl, pl], writes=[p], d=0.45 * fw.scale + 0.1)
                fw.op("act", lambda e, p=p, c=c: e.activation(out=ppT.t[:, c, 0:T], in_=p.t[:, 0:T], func=AF.Identity, scale=vcol(V_PSC + c)), reads=[p, vecs], writes=[ppT.c[c]])

        for g in range(5):
            if g < 4:
                wh = ws.next(("in", 4 + g))
                pl = plb.grab()
                for n in range(2):
                    s1(g, n, wh, pl)
                    yield cmm
                ws.done(wh)
                pend[g] = pl
            if g >= 1:
                s2(g - 1)
                yield 0.9

    def phase_D(ti):
        T = ti["T"]
        cmm = 0.22 * 16 if T > 32 else 1.5
        for j in range(4):
            tAs = []
            wA = ws.next(("brl", j))
            wgA = ws.next(("in", 8 + j))
            for mm_ in range(2):
                cs = mm_ * 128
                pg = proj_fm(wgA[1], cs, xnT, T, PS)
                pa = proj_fm(wA[1], cs, hT, T, PS)
                tA = S.grab()
                fw.op("act", lambda e, tA=tA, pg=pg: e.activation(out=tA.t[:, 0:T], in_=pg.t[:, 0:T], func=AF.Tanh, scale=0.5), reads=[pg], writes=[tA], tbl=1)
                fw.op("dve", lambda e, tA=tA, pa=pa: e.scalar_tensor_tensor(out=tA.t[:, 0:T], in0=tA.t[:, 0:T], scalar=1.0, in1=pa.t[:, 0:T], op0=ALU.add, op1=ALU.mult), reads=[tA, pa], writes=[tA])
                tAs.append(tA)
                yield cmm
            ws.done(wA)
            ws.done(wgA)
            wB = ws.next(("brp", j))
            wgB = ws.next(("in", 12 + j))
            for mm_ in range(2):
                m = 2 * j + mm_
                cs = mm_ * 128
                pg = proj_fm(wgB[1], cs, xnT, T, PS)
                pb_ = proj_fm(wB[1], cs, ppT, T, PS)
                tB = S.grab()
                tA = tAs[mm_]
                fw.op("act", lambda e, tB=tB, pg=pg: e.activation(out=tB.t[:, 0:T], in_=pg.t[:, 0:T], func=AF.Tanh, scale=0.5), reads=[pg], writes=[tB], tbl=1)
                fw.op("dve", lambda e, tB=tB, pb_=pb_: e.scalar_tensor_tensor(out=tB.t[:, 0:T], in0=tB.t[:, 0:T], scalar=1.0, in1=pb_.t[:, 0:T], op0=ALU.add, op1=ALU.mult), reads=[tB, pb_], writes=[tB])
                fw.op("dve", lambda e, tA=tA, tB=tB, m=m: e.tensor_tensor(out=mgT.t[:, m, 0:T], in0=tA.t[:, 0:T], in1=tB.t[:, 0:T], op=ALU.add), reads=[tA, tB], writes=[mgT.c[m]])
                yield cmm
            ws.done(wB)
            ws.done(wgB)

    def phase_E(ti, par):
        def epi(b, bs, h, p):
            xb = xres[par][b]
            fw.op("dve", lambda e: e.scalar_tensor_tensor(out=xb.t[0:bs, 512 * h:512 * h + 512], in0=p.t[0:bs, 0:512], scalar=0.5, in1=xb.t[0:bs, 512 * h:512 * h + 512], op0=ALU.mult, op1=ALU.add),
                  reads=[p, xb], writes=[xb])
        yield from tokproj(ti, par, mgT, 8, "out", epi, PS)

    def phase_G(ti):
        hst, hal_r, hal_p, hal_f = states[0 if ti["prompt"] else 1]
        T = ti["T"]
        NSG, L = ti["nseg"], ti["L"]
        cmm = 0.22 * 16 if T > 32 else 1.5
        pend = []

        def fin(j, cg, cvv):
            fw.op("act", lambda e: e.activation(out=cg.t[:, 0:T], in_=cg.t[:, 0:T], func=AF.Gelu_apprx_tanh), reads=[cg], writes=[cg], tbl=3)
            fw.op("dve", lambda e: e.tensor_tensor(out=actT.t[:, j, 0:T], in0=cg.t[:, 0:T], in1=cvv.t[:, 0:T], op=ALU.mult), reads=[cg, cvv], writes=[actT.c[j]])

        for jp in range(12):
            wg = ws.next(("up", jp))
            wv = ws.next(("up", 12 + jp))
            for mm_ in range(2):
                j = 2 * jp + mm_
                cs = mm_ * 128
                res = []
                for (wh, ch) in ((wg, j), (wv, 24 + j)):
                    p = proj_fm(wh[1], cs, x1nT, T, PSL)
                    hb = SL.grab()
                    hb3 = seg3(hb.t[:, 0:NSG * (2 + L)], 2 + L, NSG)
                    fw.op(HALO_IN_ENG, lambda e, hb3=hb3, ch=ch: (e.copy if HALO_IN_ENG == "act" else e.tensor_copy)(out=hb3[:, :, 0:2], in_=hal_f.t[:, :, ch, :]), reads=[hal_f.c[ch]], writes=[hb], d=0.2)
                    fw.op("act", lambda e, hb3=hb3, p=p: e.activation(out=hb3[:, :, 2:2 + L], in_=seg3(p.t[:, 0:T], L, NSG), func=AF.Copy), reads=[p], writes=[hb])
                    cv = SL.grab()
                    if (ch >= 24 and FFN_TAP_V == "dve") or (ch < 24 and FFN_TAP_G == "dve"):
                        fw.op("dve", lambda e, cv=cv, hb3=hb3, ch=ch: e.tensor_scalar(out=seg3(cv.t[:, 0:T], L, NSG), in0=hb3[:, :, 2:2 + L], scalar1=vcol(V_CFW + 3 * ch + 2), scalar2=vcol(V_CFB + ch), op0=ALU.mult, op1=ALU.add),
                              reads=[hb, vecs], writes=[cv], d=0.4 * fw.scale + 0.08)
                    else:
                        if TAP_SRC == "psum":
                            fw.op("act", lambda e, cv=cv, p=p, ch=ch: e.activation(out=cv.t[:, 0:T], in_=p.t[:, 0:T], func=AF.Identity, scale=vcol(V_CFW + 3 * ch + 2), bias=vcol(V_CFB + ch)),
                                  reads=[p, vecs], writes=[cv])
                        else:
                            fw.op("act", lambda e, cv=cv, hb3=hb3, ch=ch: e.activation(out=seg3(cv.t[:, 0:T], L, NSG), in_=hb3[:, :, 2:2 + L], func=AF.Identity, scale=vcol(V_CFW + 3 * ch + 2), bias=vcol(V_CFB + ch)),
                                  reads=[hb, vecs], writes=[cv], d=0.55 * fw.scale + 0.1)
                    for k in range(2):
                        fw.op("dve", lambda e, cv=cv, hb3=hb3, ch=ch, k=k: e.scalar_tensor_tensor(out=seg3(cv.t[:, 0:T], L, NSG), in0=hb3[:, :, k:k + L], scalar=vcol(V_CFW + 3 * ch + k),
                                                                                                in1=seg3(cv.t[:, 0:T], L, NSG), op0=ALU.mult, op1=ALU.add),
                              reads=[hb, cv, vecs], writes=[cv])
                    fw.op("act", lambda e, hb3=hb3, ch=ch: e.copy(out=hal_f.t[:, :, ch, :], in_=hb3[:, :, L:L + 2]), reads=[hb], writes=[hal_f.c[ch]], d=0.2)
                    res.append(cv)
                pend.append((j, res[0], res[1]))
                if len(pend) > 1:
                    fin(*pend.pop(0))
                yield cmm
            ws.done(wg)
            ws.done(wv)
        while pend:
            fin(*pend.pop(0))
        yield 0.1

    def phase_H(ti, par):
        blks = blocks_of(ti["T"])

        def epi(b, bs, h, p):
            xb = xres[par][b]
            fw.op("dve", lambda e: e.tensor_tensor(out=xb.t[0:bs, 512 * h:512 * h + 512], in0=p.t[0:bs, 0:512], in1=xb.t[0:bs, 512 * h:512 * h + 512], op=ALU.add),
                  reads=[p, xb], writes=[xb])
            if h == 1:
                s = norm_p1(xb, bs, statL)
                fw.op("dve", lambda e: e.scalar_tensor_tensor(out=xb.t[0:bs, :], in0=xb.t[0:bs, :], scalar=s.t[0:bs, 1:2], in1=gf.t[0:bs, :], op0=ALU.mult, op1=ALU.mult),
                      reads=[xb, s, gf], writes=[xb], d=2.1)
                c0 = blks[b][0]
                dst = ti["yd"][ti["r0"] + c0: ti["r0"] + c0 + bs, :]
                fw.dma("sp", lambda e: e.dma_start(out=dst, in_=xb.t[0:bs, :]), s_x[par][b], reads=[xb], nbytes=bs * D * 4)
        yield from tokproj(ti, par, actT, 24, "down", epi, PSL)

    xt_cnt = [0]

    def front(i):
        ti = tiles[i]
        for b, (c0, bs) in enumerate(blocks_of(ti["T"])):
            slot = xt_cnt[0] % NXT
            xt_cnt[0] += 1
            xb = xtmp[slot]
            src = ti["xd"][ti["r0"] + c0: ti["r0"] + c0 + bs, :]
            fw.dma("sp", lambda e, xb=xb, src=src, bs=bs: e.dma_start(out=xb.t[0:bs, :], in_=src), s_xt[slot], writes=[xb], nbytes=bs * D * 4)
            s = norm_p1(xb, bs, stat)
            norm_p2(xb, bs, s, V_G1, xnT, c0)
            yield 1.0
        if ti["first"]:
            init_state_early(ti)
        for v in phase_B(ti):
            yield (2.0 * v) if ti["T"] > 32 else v

    def mid(i):
        ti = tiles[i]
        par = i % 2
        blks = blocks_of(ti["T"])
        yield from phase_C(ti)
        if ti["last"]:
            store_state_early(ti)
        yield from phase_D(ti)
        load_tile(ti, par)
        yield from phase_E(ti, par)
        yield "wait_g"
        for b, (c0, bs) in enumerate(blks):
            s = norm_p1(xres[par][b], bs, stat)
            norm_p2(xres[par][b], bs, s, V_G2, x1nT, c0)
            yield 0.5

    def late(i):
        ti = tiles[i]
        par = i % 2
        if ti["first"]:
            init_state_late(ti)
        yield from phase_G(ti)
        yield "g_done"
        if ti["last"]:
            store_state_late(ti)
        yield from phase_H(ti, par)

    def scaled(gen, sc):
        while True:
            fw.scale = sc
            try:
                v = next(gen)
            except StopIteration:
                return
            yield v

    def chain(*gens):
        for g in gens:
            yield from g

    def interleave(ga, gb, key):
        tot = STAGE_TOT.get(key)
        fa, fb = (1.0 / max(tot[0], 1e-6), 1.0 / max(tot[1], 1e-6)) if tot else (1.0, 1.0)
        ta = tb_ = 0.0
        da = ga is None
        db = gb is None
        g_done = gb is None
        blocked = False
        while not (da and db):
            if blocked and g_done:
                blocked = False
            if not da and not blocked and (db or ta * fa <= tb_ * fb):
                try:
                    v = next(ga)
                    if v == "wait_g":
                        blocked = not g_done
                    else:
                        ta += v
                except StopIteration:
                    da = True
            else:
                assert not db, "deadlock in interleave"
                try:
                    v = next(gb)
                    if v == "g_done":
                        g_done = True
                    else:
                        tb_ += v
                except StopIteration:
                    db = True
                    g_done = True
        STAGE_NEW[key] = (ta, tb_)

    nt = len(tiles)
    tsc = [t_["T"] / float(TT) for t_ in tiles]

    def g_front(i):
        return scaled(front(i), tsc[i]) if i < nt else iter(())

    def g_mid(i):
        return scaled(mid(i), tsc[i]) if i < nt else iter(())

    interleave(chain(g_front(0), g_mid(0), g_front(1)), None, -1)
    for i in range(nt):
        interleave(chain(g_mid(i + 1), g_front(i + 2)), scaled(late(i), tsc[i]), i)

    if worder is None:
        st.close()
        return ws.order

    fin = [xres[p][b] for p in range(2) for b in range(4)] + [tb for q in range(2) for ct_ in states[q] for tb in ct_.c]
    fw.final_wait("sp", fin)
    fw.build(st)
    st.close()
    return nc


STAGE_TOT = {}
STAGE_NEW = {}


def build_all():
    STAGE_TOT.clear()
    build_program(None)
    STAGE_TOT.update(STAGE_NEW)
    order = build_program(None)
    return build_program(order)


def _fm(v):
    v = np.asarray(v, dtype=np.float32)
    n = v.shape[-1] // 128
    lead = v.shape[:-1]
    a = v.reshape(lead + (n, 128))
    a = np.moveaxis(a, -1, 0)
    return np.ascontiguousarray(a)


_NC_CACHE = {}


def kernel(x_prompt, x_sample, state_lru_h, state_lru_conv, state_pool, state_ffn_conv,
           norm_mix, w_in, conv_lru_w, conv_lru_b, w_ra, b_ra, w_ix, b_ix, lru_lambda,
           w_pool, pool_scale, w_br_lru, w_br_pool, w_out,
           norm_ffn, w_up, conv_ffn_w, conv_ffn_b, w_down, norm_final):
    f = lambda a: np.ascontiguousarray(np.asarray(a, dtype=np.float32))
    x_prompt, x_sample = f(x_prompt), f(x_sample)
    vecs = np.zeros((128, NV), np.float32)
    vecs[:, V_G1:V_G1 + 8] = _fm(norm_mix[0])
    vecs[:, V_G2:V_G2 + 8] = _fm(norm_ffn[0])
    vecs[:, V_CLW:V_CLW + 32] = np.transpose(_fm(conv_lru_w[0]), (0, 2, 1)).reshape(128, 32)
    vecs[:, V_CLB:V_CLB + 8] = _fm(conv_lru_b[0])
    vecs[:, V_BA:V_BA + 8] = _fm(b_ra[0])
    vecs[:, V_BX:V_BX + 8] = _fm(b_ix[0])
    vecs[:, V_LAM:V_LAM + 8] = _fm(lru_lambda[0])
    vecs[:, V_PSC:V_PSC + 8] = _fm(pool_scale[0])
    vecs[:, V_CFW:V_CFW + 144] = np.transpose(_fm(conv_ffn_w[0]), (0, 2, 1)).reshape(128, 144)
    vecs[:, V_CFB:V_CFB + 48] = _fm(conv_ffn_b[0])
    gf_bc = np.ascontiguousarray(np.broadcast_to(f(norm_final)[None, :], (128, D)))
    ident = np.eye(128, dtype=np.float32)
    corr = np.ones((128, 64), np.float32)
    for g, w in enumerate((2, 4, 8, 16)):
        for t in range(16):
            corr[:, 16 * g + t] = float(w) / float(min(t + 1, w))
    shared = dict(vecs=vecs, gf_bc=gf_bc, ident=ident, corr=corr,
                  w_in=f(w_in[0]), w_ra=f(w_ra[0]), w_ix=f(w_ix[0]), w_pool=f(w_pool[0]),
                  w_br_lru=f(w_br_lru[0]), w_br_pool=f(w_br_pool[0]), w_out=f(w_out[0]),
                  w_up=f(w_up[0]), w_down=f(w_down[0]))
    in_maps = []
    for i in range(NCORES):
        sl = slice(2 * i, 2 * i + 2)
        m = dict(shared)
        m["xp"] = x_prompt[i]
        m["xs"] = x_sample[sl].reshape(2 * DEC, D)
        m["st_h"] = np.ascontiguousarray(_fm(state_lru_h[0, sl]))
        m["st_r"] = np.ascontiguousarray(np.transpose(_fm(state_lru_conv[0, sl]), (0, 1, 3, 2)).reshape(128, 2, 24))
        m["st_p"] = np.ascontiguousarray(np.transpose(_fm(state_pool[0, sl]), (0, 1, 3, 2)).reshape(128, 2, 120))
        m["st_f"] = np.ascontiguousarray(np.transpose(_fm(state_ffn_conv[0, sl]), (0, 1, 3, 2)).reshape(128, 2, 96))
        in_maps.append(m)

    if "nc" not in _NC_CACHE:
        _NC_CACHE["nc"] = build_all()
    nc = _NC_CACHE["nc"]
    res = run_bass_kernel_spmd(nc, in_maps, core_ids=list(range(NCORES)))
    R = res.results

    def unfm(a, nch, k):
        a = np.asarray(a).reshape(128, nch, k)
        return np.ascontiguousarray(np.transpose(a, (2, 1, 0)).reshape(k, nch * 128))

    y_prompt = np.stack([np.asarray(R[i]["yp"]) for i in range(NCORES)], 0).astype(np.float32)
    y_sample = np.concatenate([np.asarray(R[i]["ys"]).reshape(2, DEC, D) for i in range(NCORES)], 0).astype(np.float32)
    p_h = np.stack([unfm(R[i]["o_h"][:, 0, :], 8, 1)[0] for i in range(NCORES)], 0)[None]
    p_lru = np.stack([unfm(R[i]["o_r"][:, 0, :], 8, 3) for i in range(NCORES)], 0)[None]
    p_pool = np.stack([unfm(R[i]["o_p"][:, 0, :], 8, 15) for i in range(NCORES)], 0)[None]
    p_ffn = np.stack([unfm(R[i]["o_f"][:, 0, :], 48, 2) for i in range(NCORES)], 0)[None]
    s_h = np.stack([unfm(R[i]["o_h"][:, 1 + s, :], 8, 1)[0] for i in range(NCORES) for s in range(2)], 0)[None]
    s_lru = np.stack([unfm(R[i]["o_r"][:, 1 + s, :], 8, 3) for i in range(NCORES) for s in range(2)], 0)[None]
    s_pool = np.stack([unfm(R[i]["o_p"][:, 1 + s, :], 8, 15) for i in range(NCORES) for s in range(2)], 0)[None]
    s_ffn = np.stack([unfm(R[i]["o_f"][:, 1 + s, :], 48, 2) for i in range(NCORES) for s in range(2)], 0)[None]
    outs = (y_prompt, y_sample, p_h, p_lru, p_pool, p_ffn, s_h, s_lru, s_pool, s_ffn)
    return tuple(np.ascontiguousarray(o, dtype=np.float32) for o in outs)
```
